# Optimizing a Trainium2 kernel written in Bass

```python
import math
import jax, jax.numpy as jnp
from jax import lax
import numpy as np


D_MODEL = 1024
BATCH = 16
SEQ = 2048
DEPTH = 1

CHUNK = 64
Q_BLOCK = 128
MLA_HEADS = 8
QK_NOPE_DIM = 64
QK_ROPE_DIM = 32
V_HEAD_DIM = 64
Q_LORA_RANK = 256
KV_LORA_RANK = 128
ATTN_WIDTH = MLA_HEADS * V_HEAD_DIM
CONV_WIDTH = D_MODEL - ATTN_WIDTH
CONV_GROUPS = 8
SHORT_CONV_K = 3
D_FF = 2816
FFN_CONV_K = 3
ROPE_THETA = 10000.0
RMS_EPS = 1e-6
LN_EPS = 1e-5
DEEPNORM_ALPHA = (2.0 * DEPTH) ** 0.25
DEEPNORM_BETA = (8.0 * DEPTH) ** -0.25
IN_PROJ_WIDTH = Q_LORA_RANK + KV_LORA_RANK + QK_ROPE_DIM + 3 * CONV_WIDTH

kernel_name = 'hybrid_mla_shortconv_convffn_block'


def layer_norm(x, g=None, b=None):
    xf = x.astype(jnp.float32)
    mu = jnp.mean(xf, axis=-1, keepdims=True)
    var = jnp.mean(jnp.square(xf - mu), axis=-1, keepdims=True)
    y = (xf - mu) * lax.rsqrt(var + LN_EPS)
    if g is not None:
        y = y * g.astype(jnp.float32) + b.astype(jnp.float32)
    return y.astype(x.dtype)


def rms_norm(x, g):
    xf = x.astype(jnp.float32)
    y = xf * lax.rsqrt(jnp.mean(jnp.square(xf), axis=-1, keepdims=True) + RMS_EPS)
    return (y * g.astype(jnp.float32)).astype(x.dtype)


def group_rms_norm(y, n_groups, g):
    B, S, W = y.shape
    yf = y.reshape(B, S, n_groups, W // n_groups).astype(jnp.float32)
    yf = yf * lax.rsqrt(jnp.mean(jnp.square(yf), axis=-1, keepdims=True) + RMS_EPS)
    return (yf.reshape(B, S, W) * g.astype(jnp.float32)).astype(y.dtype)


def rope_cos_sin(positions, dim, dtype):
    inv_freq = ROPE_THETA ** (-jnp.arange(0, dim, 2, dtype=jnp.float32) / dim)
    ang = positions.astype(jnp.float32)[..., None] * inv_freq
    return jnp.cos(ang).astype(dtype), jnp.sin(ang).astype(dtype)


def apply_rope(x, cos, sin):
    x1, x2 = jnp.split(x, 2, axis=-1)
    return jnp.concatenate([x1 * cos - x2 * sin, x2 * cos + x1 * sin], axis=-1)


def causal_dwconv(u, w, b):
    K = w.shape[0]
    S = u.shape[1]
    up = jnp.pad(u, ((0, 0), (K - 1, 0), (0, 0)))
    y = b
    for k in range(K):
        y = y + w[k] * up[:, k:k + S]
    return y


def chunk_causal_mla_attention(q_nope, q_rope, k_nope, k_rope, v):
    B, S, H, _ = q_nope.shape
    nb = S // Q_BLOCK
    scale = (QK_NOPE_DIM + QK_ROPE_DIM) ** -0.5
    key_chunk = jnp.arange(S) // CHUNK

    def to_blocks(t):
        return t.reshape((B, nb, Q_BLOCK) + t.shape[2:]).swapaxes(0, 1)

    def block(args):
        qn, qr, bi = args
        s = (jnp.einsum('bqhd,bkhd->bhqk', qn, k_nope)
             + jnp.einsum('bqhr,bkr->bhqk', qr, k_rope)).astype(jnp.float32) * scale
        q_chunk = (bi * Q_BLOCK + jnp.arange(Q_BLOCK)) // CHUNK
        allowed = key_chunk[None, :] <= q_chunk[:, None]
        s = jnp.where(allowed[None, None], s, -1e30)
        p = jax.nn.softmax(s, axis=-1).astype(v.dtype)
        return jnp.einsum('bhqk,bkhd->bqhd', p, v)

    out = lax.map(block, (to_blocks(q_nope), to_blocks(q_rope), jnp.arange(nb)))
    return out.swapaxes(0, 1).reshape(B, S, H * V_HEAD_DIM)


def hybrid_mixer(h, cos, sin, w_in, q_norm_g, w_q_up, kv_norm_g, w_kv_up,
                 conv_w, conv_b, out_norm_g, w_out):
    B, S, _ = h.shape
    proj = h @ w_in
    splits = np.cumsum([Q_LORA_RANK, KV_LORA_RANK, QK_ROPE_DIM, CONV_WIDTH, CONV_WIDTH]).tolist()
    c_q, c_kv, k_rope_raw, gate_b, gate_c, conv_v = jnp.split(proj, splits, axis=-1)

    q = (rms_norm(c_q, q_norm_g) @ w_q_up).reshape(B, S, MLA_HEADS, QK_NOPE_DIM + QK_ROPE_DIM)
    q_nope, q_rope = q[..., :QK_NOPE_DIM], q[..., QK_NOPE_DIM:]
    q_rope = apply_rope(q_rope, cos[:, :, None, :], sin[:, :, None, :])
    kv = (rms_norm(c_kv, kv_norm_g) @ w_kv_up).reshape(B, S, MLA_HEADS, QK_NOPE_DIM + V_HEAD_DIM)
    k_nope, v = kv[..., :QK_NOPE_DIM], kv[..., QK_NOPE_DIM:]
    k_rope = apply_rope(k_rope_raw, cos, sin)
    y_attn = chunk_causal_mla_attention(q_nope, q_rope, k_nope, k_rope, v)

    y_conv = gate_b * causal_dwconv(gate_c * conv_v, conv_w, conv_b)

    y = jnp.concatenate([group_rms_norm(y_attn, MLA_HEADS, out_norm_g[:ATTN_WIDTH]),
                         group_rms_norm(y_conv, CONV_GROUPS, out_norm_g[ATTN_WIDTH:])], axis=-1)
    return y @ w_out


def conv_ffn(h, w_up, ffn_conv_w, ffn_conv_b, w_down):
    u = causal_dwconv(h @ w_up, ffn_conv_w, ffn_conv_b)
    g, val = jnp.split(u, 2, axis=-1)
    return (jax.nn.silu(g) * val) @ w_down


def setup_inputs(seed: int = 0) -> dict:
    key = jax.random.key(seed)
    ks = jax.random.split(key, 24)
    L, D = DEPTH, D_MODEL

    def nrm(k, shape, scale):
        return jax.random.normal(k, shape, jnp.float32) * scale

    def gain(k, shape):
        return 1.0 + 0.02 * jax.random.normal(k, shape, jnp.float32)

    offsets = jax.random.randint(ks[2], (BATCH,), 0, 4096, dtype=jnp.int32)
    positions = offsets[:, None] + jnp.arange(SEQ, dtype=jnp.int32)[None, :]
    return {
        'x': nrm(ks[0], (BATCH, SEQ, D), 1.0),
        'c': nrm(ks[1], (BATCH, D), 1.0),
        'positions': positions,
        'w_ada': nrm(ks[3], (L, D, 6 * D), D ** -0.5),
        'b_ada': nrm(ks[4], (L, 6 * D), 0.02),
        'w_in': nrm(ks[5], (L, D, IN_PROJ_WIDTH), D ** -0.5),
        'q_norm_g': gain(ks[6], (L, Q_LORA_RANK)),
        'w_q_up': nrm(ks[7], (L, Q_LORA_RANK, MLA_HEADS * (QK_NOPE_DIM + QK_ROPE_DIM)), Q_LORA_RANK ** -0.5),
        'kv_norm_g': gain(ks[8], (L, KV_LORA_RANK)),
        'w_kv_up': nrm(ks[9], (L, KV_LORA_RANK, MLA_HEADS * (QK_NOPE_DIM + V_HEAD_DIM)), KV_LORA_RANK ** -0.5),
        'conv_w': nrm(ks[10], (L, SHORT_CONV_K, CONV_WIDTH), SHORT_CONV_K ** -0.5),
        'conv_b': nrm(ks[11], (L, CONV_WIDTH), 0.02),
        'out_norm_g': gain(ks[12], (L, D)),
        'w_out': nrm(ks[13], (L, D, D), D ** -0.5 * DEEPNORM_BETA),
        'ln1_g': gain(ks[14], (L, D)),
        'ln1_b': nrm(ks[15], (L, D), 0.02),
        'w_up': nrm(ks[16], (L, D, 2 * D_FF), D ** -0.5),
        'ffn_conv_w': nrm(ks[17], (L, FFN_CONV_K, 2 * D_FF), FFN_CONV_K ** -0.5),
        'ffn_conv_b': nrm(ks[18], (L, 2 * D_FF), 0.02),
        'w_down': nrm(ks[19], (L, D_FF, D), D_FF ** -0.5 * DEEPNORM_BETA),
        'ln2_g': gain(ks[20], (L, D)),
        'ln2_b': nrm(ks[21], (L, D), 0.02),
    }


def reference(x, c, positions, w_ada, b_ada, w_in, q_norm_g, w_q_up, kv_norm_g, w_kv_up,
              conv_w, conv_b, out_norm_g, w_out, ln1_g, ln1_b, w_up, ffn_conv_w, ffn_conv_b,
              w_down, ln2_g, ln2_b):
    cos, sin = rope_cos_sin(positions, QK_ROPE_DIM, x.dtype)
    c_act = jax.nn.silu(c)
    for l in range(DEPTH):
        mod = c_act @ w_ada[l] + b_ada[l]
        shift_m, scale_m, gate_m, shift_f, scale_f, gate_f = [m[:, None, :] for m in jnp.split(mod, 6, axis=-1)]
        h = layer_norm(x) * (1.0 + scale_m) + shift_m
        mix = hybrid_mixer(h, cos, sin, w_in[l], q_norm_g[l], w_q_up[l], kv_norm_g[l], w_kv_up[l],
                           conv_w[l], conv_b[l], out_norm_g[l], w_out[l])
        x = layer_norm(DEEPNORM_ALPHA * x + gate_m * mix, ln1_g[l], ln1_b[l])
        h = layer_norm(x) * (1.0 + scale_f) + shift_f
        ff = conv_ffn(h, w_up[l], ffn_conv_w[l], ffn_conv_b[l], w_down[l])
        x = layer_norm(DEEPNORM_ALPHA * x + gate_f * ff, ln2_g[l], ln2_b[l])
    return x
```

```python
import numpy as np
from contextlib import ExitStack
import concourse.bass as bass
import concourse.mybir as mybir
from concourse.bass_utils import run_bass_kernel_spmd

F32 = mybir.dt.float32
BF16 = mybir.dt.bfloat16
I32 = mybir.dt.int32
AF = mybir.ActivationFunctionType
ALU = mybir.AluOpType

D = 1024
SEQ = 2048
NBC = 2
TOK = NBC * SEQ
T1 = 256
NT1 = TOK // T1
T2 = 512
NT2 = TOK // T2
NH = 8
DFF = 2816
NPAIR = DFF // 128
ALPHA = 2.0 ** 0.25
LN_EPS = 1e-5
RMS_EPS = 1e-6
SCALE = 96.0 ** -0.5
QK_REP = 1

C_BADA = 0
C_QG = 48
C_KVG = 50
C_ONGA = 51
C_ONGC = 59
C_CW = 63
C_CB = 75
C_FCW = 79
C_FCB = 211
C_INVF = 255
C_CT = 256
NPP = 272

TWO_PI = float(2 * np.pi)
CW1 = 6.28125
CW2 = TWO_PI - CW1


class Buf:
    __slots__ = ("name", "w", "r", "dsem", "dval")

    def __init__(self, name):
        self.name = name
        self.w = None
        self.r = {}
        self.dsem = None
        self.dval = 0


class Sched:
    def __init__(self, nc, es):
        self.nc = nc
        self.es = es
        self.eng = {"pe": nc.tensor, "act": nc.scalar, "dve": nc.vector, "pool": nc.gpsimd, "sp": nc.sync}
        self.sem = {k: es.enter_context(nc.semaphore("s_" + k)) for k in self.eng}
        self.cnt = {k: 0 for k in self.eng}
        self.seen = {k: {} for k in self.eng}
        self.dma_last = {}

    def _wait(self, e, tok):
        if tok is None:
            return
        sem, val = tok
        if self.seen[e].get(sem.num, 0) >= val:
            return
        self.eng[e].wait_ge(sem, val)
        self.seen[e][sem.num] = val

    def _deps(self, e, reads, writes):
        for b in reads:
            self._wait(e, b.w)
        for b in writes:
            self._wait(e, b.w)
            for t in list(b.r.values()):
                self._wait(e, t)

    @staticmethod
    def _commit(key, tok, reads, writes):
        for b in reads:
            b.r[key] = tok
        for b in writes:
            b.w = tok
            b.r = {}

    def op(self, e, fn, reads=(), writes=()):
        self._deps(e, reads, writes)
        ins = fn(self.eng[e])
        self.cnt[e] += 1
        ins.then_inc(self.sem[e], 1)
        tok = (self.sem[e], self.cnt[e])
        self._commit(e, tok, reads, writes)
        return tok

    def dma(self, q, out, in_, reads=(), writes=(), owner=None, group=False, **kw):
        ow = owner or (writes[0] if writes else reads[0])
        if ow.dsem is None:
            ow.dsem = self.es.enter_context(self.nc.semaphore("d_" + ow.name))
        for b in reads:
            self._wait(q, b.w)
        for b in writes:
            if not (group and b.w is not None and b.w[0] is ow.dsem):
                self._wait(q, b.w)
            for t in list(b.r.values()):
                self._wait(q, t)
        ow.dval += 16
        self.eng[q].dma_start(out=out, in_=in_, **kw).then_inc(ow.dsem, 16)
        tok = (ow.dsem, ow.dval)
        self._commit(("dma", ow.dsem.num), tok, reads, writes)
        self.dma_last[ow.dsem.num] = tok
        return tok

    def barrier(self):
        toks = [(self.sem[k], self.cnt[k]) for k in self.eng if self.cnt[k] > 0]
        toks += list(self.dma_last.values())
        for e in self.eng:
            for t in toks:
                if t[0] is self.sem[e]:
                    continue
                self._wait(e, t)


class Rot:
    def __init__(self, items):
        self.items = items
        self.i = 0

    def next(self):
        it = self.items[self.i % len(self.items)]
        self.i += 1
        return it


def build(dbg=False):
    nc = bass.Bass("TRN2", target_bir_lowering=False)

    def din(name, shape, dt=F32):
        return nc.dram_tensor(name, shape, dt, kind="ExternalInput").ap()

    x_d = din("x", [TOK, D])
    pos_d = din("pos", [1, TOK], I32)
    pp_d = din("pp", [128, NPP])
    wada_d = din("w_ada", [128, 8, 6144])
    win_d = din("w_in", [128, 8, 1952])
    wq_d = din("wq", [128, 2, 768])
    wkv_d = din("wkv", [128, 1024])
    woa_d = din("woa", [128, 4, 1024])
    woc_d = din("woc", [128, 4, 1024])
    wup_d = din("wup", [NPAIR, 128, 8, 2, 128])
    wdn_d = din("wdn", [128, NPAIR, 1024])
    lnp_d = din("lnp", [4, 1024])
    ident_d = din("ident", [128, 128])
    out_d = nc.dram_tensor("out", [TOK, D], F32, kind="ExternalOutput").ap()
    x1s_d = nc.dram_tensor("x1s", [TOK, D], F32, kind="ExternalOutput" if dbg else "Internal").ap()
    cs_d = nc.dram_tensor("cs", [2, 32, TOK], F32, kind="Internal").ap()
    st2_d = nc.dram_tensor("st2", [TOK // 128, 128, 2], F32, kind="Internal").ap()
    gsc_d = nc.dram_tensor("gsc", [4, 1024], F32, kind="Internal").ap()
    wupb_d = nc.dram_tensor("wupb", [NPAIR, 128, 2048], BF16, kind="Internal").ap()
    wdnb_d = nc.dram_tensor("wdnb", [128, NPAIR, 1024], BF16, kind="Internal").ap()

    with ExitStack() as es:
        S = Sched(nc, es)

        def sbt(stack, name, shape, dt=F32):
            return stack.enter_context(nc.sbuf_tensor("sb_" + name, shape, dt))

        pp = sbt(es, "pp", [128, NPP]); B_pp = Buf("pp")
        identb = sbt(es, "identb", [128, 128], BF16); B_idb = Buf("idb")
        onesb = sbt(es, "onesb", [128, 128], BF16)
        wn = sbt(es, "wn", [128, 64], BF16)
        bd = sbt(es, "bd", [128, 128], BF16)
        B_const = Buf("const")
        modT = sbt(es, "modT", [128, 48, 2]); B_mod = Buf("mod")
        lnb = sbt(es, "lnb", [128, 2, 1024]); B_lnb = Buf("lnb")
        gbc = sbt(es, "gbc", [128, 1024]); B_gbc = Buf("gbc")
        xnb = [sbt(es, f"xnb{i}", [128, 1024], BF16) for i in range(2)]
        xn_rot = Rot([(xnb[i], Buf(f"xn{i}")) for i in range(2)])
        stt_ = [sbt(es, f"st{i}", [128, 2, 6]) for i in range(2)]
        mvt_ = [sbt(es, f"mv{i}", [128, 2]) for i in range(2)]
        rst_ = [sbt(es, f"rs{i}", [128, 2]) for i in range(2)]
        st_rot = Rot([(stt_[i], mvt_[i], rst_[i], Buf(f"st{i}"), Buf(f"mv{i}"), Buf(f"rs{i}")) for i in range(2)])

        tpb = [es.enter_context(nc.psum_tensor(f"tp{i}", [128, 1024], BF16)) for i in range(1)]
        tp_rot = Rot([(tpb[i], Buf(f"tp{i}")) for i in range(1)])
        mmb = [es.enter_context(nc.psum_tensor(f"mm{i}", [128, 512], F32)) for i in range(7)]
        B_mm = [Buf(f"mm{i}") for i in range(7)]

        S.dma("sp", pp[:], pp_d, writes=[B_pp])
        S.dma("pool", identb[:], ident_d, writes=[B_idb])

        def f_const(v):
            v.memset(onesb[:], 1.0)
            v.memset(wn[0:64, :], 1.0)
            v.memset(wn[64:128, :], 64.0 * RMS_EPS)
            return v.memset(bd[:], 0.0)
        S.op("dve", f_const, writes=[B_const])

        def f_const2(v):
            v.memset(bd[0:64, 0:64], 1.0)
            return v.memset(bd[64:128, 64:128], 1.0)
        S.op("dve", f_const2, writes=[B_const])

        with ExitStack() as es1:
            w_in = sbt(es1, "w_in", [128, 8, 1952], BF16); B_win = Buf("win")
            wkr = sbt(es1, "wkr", [128, 8, 2, 96], BF16); B_wkr = Buf("wkr")
            wq = sbt(es1, "wq", [128, 2, 768], BF16); B_wq = Buf("wq")
            wqr = sbt(es1, "wqr", [128, 2, 8, 96], BF16); B_wqr = Buf("wqr")
            wkv = sbt(es1, "wkv", [128, 1024], BF16); B_wkv = Buf("wkv")
            woa = sbt(es1, "woa", [128, 4, 1024], BF16); B_woa = Buf("woa")
            woc = sbt(es1, "woc", [128, 4, 1024], BF16); B_woc = Buf("woc")
            kT = sbt(es1, "kT", [128, 8, SEQ], BF16)
            vaug = sbt(es1, "vaug", [128, SEQ // 128, 8, 65], BF16)
            B_kTn = [[Buf(f"kTn{j}_{hp}") for hp in range(4)] for j in range(8)]
            B_kTr = [Buf(f"kTr{j}") for j in range(8)]
            B_v = [[Buf(f"v{j}_{s}") for s in range(2)] for j in range(8)]
            B_vinit = Buf("vinit")

            for k in range(0, 8, 2):
                S.dma("pool", w_in[:, k:k + 2, :], win_d[:, k:k + 2, :], writes=[B_win], group=True)
            S.dma("pool", wq[:], wq_d, writes=[B_wq])
            S.dma("pool", wkv[:], wkv_d, writes=[B_wkv])

            with ExitStack() as es0:
                cactb = sbt(es0, "cactb", [128, 16], BF16); B_cact = Buf("cact")
                wa = [sbt(es0, f"wa{i}", [128, 8, 512], BF16) for i in range(2)]
                B_wa = [Buf(f"wa{i}") for i in range(2)]
                pit = sbt(es0, "pit", [128, 1024], I32); B_pit = Buf("pit")
                ang = sbt(es0, "ang", [128, 1024]); B_ang = Buf("ang")
                kf = sbt(es0, "kf", [128, 1024]); B_kf = Buf("kf")
                kit = sbt(es0, "kit", [128, 1024], I32); B_kit = Buf("kit")
                snt = sbt(es0, "snt", [128, 1024]); B_snt = Buf("snt")
                cst = sbt(es0, "cst", [128, 1024]); B_cst = Buf("cst")

                R = slice(64, 96)
                for ch in range(TOK // 1024):
                    c0 = ch * 1024
                    S.dma("sp", pit[R, :], pos_d[0:1, c0:c0 + 1024].partition_broadcast(32), writes=[B_pit])
                    S.op("dve", lambda v: v.tensor_copy(out=ang[R, :], in_=pit[R, :]), reads=[B_pit], writes=[B_ang])
                    S.op("dve", lambda v: v.tensor_scalar(out=ang[R, :], in0=ang[R, :], scalar1=pp[R, C_INVF:C_INVF + 1],
                                                          scalar2=None, op0=ALU.mult), reads=[B_ang, B_pp], writes=[B_ang])
                    S.op("dve", lambda v: v.tensor_scalar(out=kf[R, :], in0=ang[R, :], scalar1=1.0 / TWO_PI, scalar2=None,
                                                          op0=ALU.mult), reads=[B_ang], writes=[B_kf])
                    S.op("dve", lambda v: v.tensor_copy(out=kit[R, :], in_=kf[R, :]), reads=[B_kf], writes=[B_kit])
                    S.op("dve", lambda v: v.tensor_copy(out=kf[R, :], in_=kit[R, :]), reads=[B_kit], writes=[B_kf])
                    S.op("dve", lambda v: v.scalar_tensor_tensor(out=ang[R, :], in0=kf[R, :], scalar=-CW1, in1=ang[R, :],
                                                                 op0=ALU.mult, op1=ALU.add), reads=[B_kf, B_ang], writes=[B_ang])
                    S.op("dve", lambda v: v.scalar_tensor_tensor(out=ang[R, :], in0=kf[R, :], scalar=-CW2, in1=ang[R, :],
                                                                 op0=ALU.mult, op1=ALU.add), reads=[B_kf, B_ang], writes=[B_ang])
                    S.op("dve", lambda v: v.tensor_scalar(out=kf[R, :], in0=ang[R, :], scalar1=float(np.pi), scalar2=None,
                                                          op0=ALU.is_gt), reads=[B_ang], writes=[B_kf])
                    S.op("dve", lambda v: v.scalar_tensor_tensor(out=ang[R, :], in0=kf[R, :], scalar=-TWO_PI, in1=ang[R, :],
                                                                 op0=ALU.mult, op1=ALU.add), reads=[B_kf, B_ang], writes=[B_ang])
                    S.op("dve", lambda v: v.tensor_scalar(out=ang[R, :], in0=ang[R, :], scalar1=float(np.pi), scalar2=-float(np.pi),
                                                          op0=ALU.min, op1=ALU.max), reads=[B_ang], writes=[B_ang])
                    S.op("act", lambda a: a.activation(out=snt[R, :], in_=ang[R, :], func=AF.Sin), reads=[B_ang], writes=[B_snt])
                    S.op("dve", lambda v: v.scalar_tensor_tensor(out=kf[R, :], in0=ang[R, :], scalar=-1.0, in1=ang[R, :],
                                                                 op0=ALU.mult, op1=ALU.max), reads=[B_ang], writes=[B_kf])
                    S.op("act", lambda a: a.activation(out=cst[R, :], in_=kf[R, :], func=AF.Sin, scale=-1.0,
                                                       bias=float(np.pi / 2)), reads=[B_kf], writes=[B_cst])
                    S.dma("sp", cs_d[0, :, c0:c0 + 1024], cst[R, :], reads=[B_cst])
                    S.dma("sp", cs_d[1, :, c0:c0 + 1024], snt[R, :], reads=[B_snt])

                S.op("act", lambda a: a.activation(out=cactb[:], in_=pp[:, C_CT:C_CT + 16], func=AF.Silu),
                     reads=[B_pp], writes=[B_cact])
                modps = mmb[0]
                cact3 = cactb[:].rearrange("p (k b) -> p k b", b=2)
                for jc in range(12):
                    sl = jc % 2
                    S.dma("pool", wa[sl][:], wada_d[:, :, jc * 512:(jc + 1) * 512], writes=[B_wa[sl]])

                    def f_mod(pe, jc=jc, sl=sl):
                        ins = None
                        for f in range(4):
                            j = jc * 4 + f
                            for k in range(8):
                                ins = pe.matmul(modps[:, 2 * j:2 * j + 2], lhsT=wa[sl][:, k, f * 128:(f + 1) * 128],
                                                rhs=cact3[:, k, :], start=(k == 0), stop=(k == 7))
                        return ins
                    S.op("pe", f_mod, reads=[B_wa[sl], B_cact], writes=[B_mm[0]])
                modps3 = modps[:, 0:96].rearrange("p (j b) -> p j b", b=2)

                def f_modT(v):
                    v.tensor_tensor(out=modT[:, :, 0], in0=modps3[:, :, 0], in1=pp[:, C_BADA:C_BADA + 48], op=ALU.add)
                    return v.tensor_tensor(out=modT[:, :, 1], in0=modps3[:, :, 1], in1=pp[:, C_BADA:C_BADA + 48], op=ALU.add)
                S.op("dve", f_modT, reads=[B_mm[0], B_pp], writes=[B_mod])

                def f_one(v):
                    v.tensor_scalar(out=modT[:, 8:16, :], in0=modT[:, 8:16, :], scalar1=1.0, scalar2=None, op0=ALU.add)
                    return v.tensor_scalar(out=modT[:, 32:40, :], in0=modT[:, 32:40, :], scalar1=1.0, scalar2=None, op0=ALU.add)
                S.op("dve", f_one, reads=[B_mod], writes=[B_mod])
                B_gsc = Buf("gsc")
                with nc.allow_non_contiguous_dma(reason="tiny one-time gate relayout"):
                    for b in range(NBC):
                        for wi, cbase in enumerate((16, 40)):
                            S.dma("sp", gsc_d[b * 2 + wi, :].rearrange("(k p) -> p k", p=128), modT[:, cbase:cbase + 8, b],
                                  reads=[B_mod], writes=[B_gsc], owner=B_gsc)

                S.op("dve", lambda v: v.memset(wkr[:], 0.0), writes=[B_wkr])

                def f_wkr(v):
                    v.tensor_copy(out=wkr[:, :, 0, 64:96], in_=w_in[:, :, 384:416])
                    v.tensor_scalar(out=wkr[:, :, 1, 64:80], in0=w_in[:, :, 400:416], scalar1=-1.0, scalar2=None, op0=ALU.mult)
                    return v.tensor_copy(out=wkr[:, :, 1, 80:96], in_=w_in[:, :, 384:400])
                S.op("dve", f_wkr, reads=[B_win], writes=[B_wkr])
                S.op("dve", lambda v: v.memset(wqr[:], 0.0), writes=[B_wqr])

                def f_wqr(v):
                    ins = None
                    for k in range(2):
                        wq4 = wq[:, k, :].rearrange("p (h d) -> p h d", d=96)
                        v.tensor_scalar(out=wqr[:, k, :, 64:80], in0=wq4[:, :, 80:96], scalar1=-1.0, scalar2=None, op0=ALU.mult)
                        ins = v.tensor_copy(out=wqr[:, k, :, 80:96], in_=wq4[:, :, 64:80])
                    return ins
                S.op("dve", f_wqr, reads=[B_wq], writes=[B_wqr])
                S.op("dve", lambda v: v.memset(vaug[:], 1.0), writes=[B_vinit])
                S.barrier()

            S.dma("pool", woa[:], woa_d, writes=[B_woa])
            S.dma("pool", woc[:], woc_d, writes=[B_woc])
            B_wupb = Buf("wupb"); B_wdnb = Buf("wdnb")
            for p in range(NPAIR):
                S.dma("pool", wupb_d[p], wup_d[p].rearrange("p k g n -> p (k g n)"), writes=[B_wupb])
            for j in range(0, NPAIR, 2):
                S.dma("pool", wdnb_d[:, j:j + 2, :], wdn_d[:, j:j + 2, :], writes=[B_wdnb])
            S.dma("sp", lnb[:, 0, :], lnp_d[0:1, :].partition_broadcast(128), writes=[B_lnb])
            S.dma("sp", lnb[:, 1, :], lnp_d[1:2, :].partition_broadcast(128), writes=[B_lnb], group=True)

            with ExitStack() as esa:
                xts = [sbt(esa, f"xt{i}", [128, 1024]) for i in range(4)]
                x_rot = Rot([(xts[i], Buf(f"x{i}")) for i in range(4)])
                x1st = [sbt(esa, f"x1st{i}", [128, 1024]) for i in range(2)]
                x1_rot = Rot([(x1st[i], Buf(f"x1st{i}")) for i in range(2)])
                hT = sbt(esa, "hT", [128, 8, T1], BF16); B_hT = Buf("hT")
                cqf = sbt(esa, "cqf", [128, 3, T1]); B_cqf = Buf("cqf")
                sq = sbt(esa, "sq", [128, 3, T1], BF16); B_sq = Buf("sq")
                rstd = sbt(esa, "rstd", [128, 2, T1]); B_rstd = Buf("rstd")
                cqn = sbt(esa, "cqn", [128, 3, T1], BF16); B_cqn = Buf("cqn")
                cos_t = sbt(esa, "cos_t", [128, T1]); sin_t = sbt(esa, "sin_t", [128, T1]); B_cs = Buf("cs")
                t1 = sbt(esa, "t1", [128, T1]); B_t1 = Buf("t1")
                t2 = sbt(esa, "t2", [128, T1]); B_t2 = Buf("t2")
                krb = sbt(esa, "krb", [128, T1], BF16); B_krb = Buf("krb")
                gcs = sbt(esa, "gcs", [128, T1]); B_gcs = Buf("gcs")
                ub = sbt(esa, "ub", [128, 4, T1 + 2]); B_u = [Buf(f"u{i}") for i in range(4)]
                cacc = sbt(esa, "cacc", [128, T1]); B_cacc = Buf("cacc")
                yc = sbt(esa, "yc", [128, T1]); B_yc = Buf("yc")
                sqc = sbt(esa, "sqc", [128, T1], BF16); B_sqc = Buf("sqc")
                rstdc = sbt(esa, "rstdc", [128, T1]); B_rstdc = Buf("rstdc")
                ycT = [sbt(esa, f"ycT{i}", [128, 4, T1], BF16) for i in range(2)]; B_ycT = [Buf(f"ycT{i}") for i in range(2)]
                qT = [sbt(esa, f"qT{i}", [128, 8, T1], BF16) for i in range(2)]
                B_qT = [[Buf(f"qT{i}_{h}") for h in range(8)] for i in range(2)]
                pTs = [sbt(esa, f"pT{i}", [128, 2 * T1], BF16) for i in range(3)]
                p_rot = Rot([(pTs[i], Buf(f"pT{i}")) for i in range(3)])
                pTd = sbt(esa, "pTd", [128, 2 * T1], BF16); B_pTd = Buf("pTd")
                osb = [sbt(esa, f"osb{i}", [128, T1]) for i in range(2)]
                osq = [sbt(esa, f"osq{i}", [128, T1], BF16) for i in range(2)]
                B_osb = [Buf(f"osb{i}") for i in range(2)]
                B_osq = [Buf(f"osq{i}") for i in range(2)]
                orstd = sbt(esa, "orstd", [128, T1]); B_orstd = Buf("orstd")
                yaT = sbt(esa, "yaT", [128, 4, T1], BF16); B_yaT = [Buf(f"yaT{h}") for h in range(8)]
                ystg = [sbt(esa, f"ystg{i}", [64, T1], BF16) for i in range(2)]; B_ystg = [Buf(f"ystg{i}") for i in range(2)]
                rs2 = sbt(esa, "rs2", [128, 2, 2]); B_rs2 = Buf("rs2")
                epst = sbt(esa, "epst", [128, 2]); B_eps = Buf("eps")
                stb_ = [sbt(esa, f"stb{i}", [128, 2, 6]) for i in range(2)]
                mvb_ = [sbt(esa, f"mvb{i}", [128, 2]) for i in range(2)]
                rsb_ = [sbt(esa, f"rsb{i}", [128, 2]) for i in range(2)]
                st_rot_b = Rot([(stb_[i], mvb_[i], rsb_[i], Buf(f"stb{i}"), Buf(f"mvb{i}"), Buf(f"rsb{i}")) for i in range(2)])

                F0, F1, F2 = mmb[0], mmb[1], mmb[2]
                B_F0, B_F2 = B_mm[0], B_mm[2]
                B_F1a = B_F1b = B_mm[1]
                mmB = Rot([(mmb[3], B_mm[3]), (mmb[4], B_mm[4]), (mmb[2], B_mm[2])])
                qkv_rot = Rot([(F0, B_F0), (F1, B_F1a)])
                accbs = [mmb[5], mmb[6]]
                B_acc = [B_mm[5], B_mm[6]]

                S.op("dve", lambda v: v.memset(pTd[:], 0.0), writes=[B_pTd])
                S.op("dve", lambda v: v.memset(epst[:, 0:1], LN_EPS), writes=[B_eps])
                S.op("dve", lambda v: v.memset(epst[:, 1:2], RMS_EPS), writes=[B_eps])

                def ln_stats_a(src, Bsrc, pool):
                    st, mv, rs, B_st, B_mv, B_rs = pool.next()

                    def f1(v):
                        v.bn_stats(out=st[:, 0, :], in_=src[:, 0:512])
                        return v.bn_stats(out=st[:, 1, :], in_=src[:, 512:1024])
                    S.op("dve", f1, reads=[Bsrc], writes=[B_st])
                    S.op("dve", lambda v: v.bn_aggr(out=mv[:], in_=st[:]), reads=[B_st], writes=[B_mv])
                    return rs, B_rs, mv, B_mv

                def ln_stats_b(h, want_nb=False):
                    rs, B_rs, mv, B_mv = h
                    S.op("act", lambda a: a.activation(out=rs[:, 0:1], in_=mv[:, 1:2], func=AF.Ln, bias=epst[:, 0:1], scale=1.0),
                         reads=[B_mv, B_eps], writes=[B_rs])
                    S.op("act", lambda a: a.activation(out=rs[:, 0:1], in_=rs[:, 0:1], func=AF.Exp, scale=-0.5),
                         reads=[B_rs], writes=[B_rs])
                    if want_nb:
                        S.op("dve", lambda v: v.scalar_tensor_tensor(out=rs[:, 1:2], in0=mv[:, 0:1], scalar=-1.0, in1=rs[:, 0:1],
                                                                     op0=ALU.mult, op1=ALU.mult), reads=[B_mv, B_rs], writes=[B_rs])

                xs_of = {}
                R = slice(64, 96)

                def front(tile):
                    b = tile // 8
                    jt = tile % 8
                    par = tile % 2
                    t0 = jt * T1
                    g0 = b * SEQ + t0
                    if jt == 0:
                        S.op("dve", lambda v: v.memset(ub[:, :, 0:2], 0.0), writes=B_u)
                    S.dma("sp", cos_t[64:96, :], cs_d[0, :, g0:g0 + T1], writes=[B_cs])
                    S.dma("sp", sin_t[64:96, :], cs_d[1, :, g0:g0 + T1], writes=[B_cs], group=True)
                    xs = []
                    for s in range(2):
                        xt, B_x = x_rot.next()
                        xs.append((xt, B_x))
                        S.dma("sp", xt[:], x_d[g0 + s * 128:g0 + (s + 1) * 128, :], writes=[B_x])
                    xs_of[tile] = xs
                    yield
                    hs = [ln_stats_a(xs[s][0], xs[s][1], st_rot) for s in range(2)]
                    for _ in range(5):
                        yield
                    for s in range(2):
                        ln_stats_b(hs[s])
                    yield
                    for s in range(2):
                        xt, B_x = xs[s]
                        rs, B_rs, mv, B_mv = hs[s]
                        xn, B_xn = xn_rot.next()
                        S.op("dve", lambda v: v.tensor_scalar(out=xn[:], in0=xt[:], scalar1=mv[:, 0:1], scalar2=rs[:, 0:1],
                                                              op0=ALU.subtract, op1=ALU.mult),
                             reads=[B_x, B_rs, B_mv], writes=[B_xn])
                        tp, B_tp = tp_rot.next()

                        def f_tp(pe):
                            ins = None
                            for k in range(8):
                                ins = pe.transpose(out=tp[:, k * 128:(k + 1) * 128], in_=xn[:, k * 128:(k + 1) * 128], identity=identb[:])
                            return ins
                        S.op("pe", f_tp, reads=[B_xn, B_idb], writes=[B_tp])
                        yield

                        def f_ev(v):
                            ins = None
                            for k in range(8):
                                ins = v.tensor_scalar(out=hT[:, k, s * 128:(s + 1) * 128], in0=tp[:, k * 128:(k + 1) * 128],
                                                      scalar1=modT[:, 8 + k, b:b + 1], scalar2=modT[:, k, b:b + 1], op0=ALU.mult, op1=ALU.add)
                            return ins
                        S.op("dve", f_ev, reads=[B_tp, B_mod], writes=[B_hT])
                        yield

                    def inproj(pe, col0, ncols, out_ap):
                        ins = None
                        for k in range(8):
                            ins = pe.matmul(out_ap, lhsT=w_in[:, k, col0:col0 + ncols], rhs=hT[:, k, :], start=(k == 0), stop=(k == 7))
                        return ins

                    bA, B_A = F0, B_F0

                    def f_A(pe):
                        inproj(pe, 0, 128, bA[:, 0:T1])
                        return inproj(pe, 128, 128, bA[:, T1:2 * T1])
                    S.op("pe", f_A, reads=[B_win, B_hT], writes=[B_A])
                    bB, B_B = F1, B_F1a
                    S.op("pe", lambda pe: inproj(pe, 256, 128, bB[:, 0:T1]), reads=[B_win, B_hT], writes=[B_B])
                    yield
                    cqf2 = cqf[:, 0:2, :].rearrange("p a t -> p (a t)")
                    sq2 = sq[:, 0:2, :].rearrange("p a t -> p (a t)")

                    def f_evA(a):
                        a.activation(out=cqf2, in_=bA[:, :], func=AF.Copy)
                        return a.activation(out=sq2, in_=bA[:, :], func=AF.Square)
                    S.op("act", f_evA, reads=[B_A], writes=[B_cqf, B_sq])

                    def f_evB(a):
                        a.activation(out=cqf[:, 2, :], in_=bB[:, 0:T1], func=AF.Copy)
                        return a.activation(out=sq[:, 2, :], in_=bB[:, 0:T1], func=AF.Square)
                    S.op("act", f_evB, reads=[B_B], writes=[B_cqf, B_sq])
                    yield
                    bC, B_C = F1, B_F1a

                    def f_C(pe):
                        pe.matmul(bC[:, 0:T1], lhsT=onesb[:], rhs=sq[:, 0, :], start=True, stop=False)
                        pe.matmul(bC[:, 0:T1], lhsT=onesb[:], rhs=sq[:, 1, :], start=False, stop=True)
                        return pe.matmul(bC[:, T1:2 * T1], lhsT=onesb[:], rhs=sq[:, 2, :], start=True, stop=True)
                    S.op("pe", f_C, reads=[B_sq, B_const], writes=[B_C])

                    bD, B_D = F0, B_F0

                    def f_D(pe):
                        ins = None
                        for r in range(2):
                            for k in range(8):
                                ins = pe.matmul(bD[0:96, r * T1:(r + 1) * T1], lhsT=wkr[:, k, r, :], rhs=hT[:, k, :],
                                                start=(k == 0), stop=(k == 7))
                        return ins
                    S.op("pe", f_D, reads=[B_wkr, B_hT], writes=[B_D])
                    yield

                    def f_rq(a):
                        a.activation(out=rstd[:, 0, :], in_=bC[:, 0:T1], func=AF.Ln, bias=epst[:, 1:2], scale=1.0 / 256)
                        return a.activation(out=rstd[:, 1, :], in_=bC[:, T1:2 * T1], func=AF.Ln, bias=epst[:, 1:2], scale=1.0 / 128)
                    S.op("act", f_rq, reads=[B_C, B_eps], writes=[B_rstd])
                    rstd2 = rstd[:].rearrange("p a t -> p (a t)")
                    S.op("act", lambda a: a.activation(out=rstd2, in_=rstd2, func=AF.Exp, scale=-0.5), reads=[B_rstd], writes=[B_rstd])
                    yield
                    S.op("dve", lambda v: v.tensor_tensor(out=t1[R, :], in0=bD[R, 0:T1], in1=cos_t[R, :], op=ALU.mult),
                         reads=[B_D, B_cs], writes=[B_t1])
                    S.op("dve", lambda v: v.tensor_tensor(out=t2[R, :], in0=bD[R, T1:2 * T1], in1=sin_t[R, :], op=ALU.mult),
                         reads=[B_D, B_cs], writes=[B_t2])
                    yield

                    def f_cqn(v):
                        v.scalar_tensor_tensor(out=cqn[:, 0, :], in0=cqf[:, 0, :], scalar=pp[:, C_QG:C_QG + 1], in1=rstd[:, 0, :],
                                               op0=ALU.mult, op1=ALU.mult)
                        v.scalar_tensor_tensor(out=cqn[:, 1, :], in0=cqf[:, 1, :], scalar=pp[:, C_QG + 1:C_QG + 2], in1=rstd[:, 0, :],
                                               op0=ALU.mult, op1=ALU.mult)
                        return v.scalar_tensor_tensor(out=cqn[:, 2, :], in0=cqf[:, 2, :], scalar=pp[:, C_KVG:C_KVG + 1], in1=rstd[:, 1, :],
                                                      op0=ALU.mult, op1=ALU.mult)
                    S.op("dve", f_cqn, reads=[B_cqf, B_rstd, B_pp], writes=[B_cqn])
                    S.op("dve", lambda v: v.tensor_tensor(out=krb[R, :], in0=t1[R, :], in1=t2[R, :], op=ALU.add),
                         reads=[B_t1, B_t2], writes=[B_krb])
                    yield

                    def f_krc(v):
                        ins = None
                        for h in range(8):
                            ins = v.tensor_copy(out=kT[R, h, t0:t0 + T1], in_=krb[R, :])
                        return ins
                    S.op("dve", f_krc, reads=[B_krb], writes=[B_kTr[jt]])
                    yield

                    def qkv_gen():
                        for h in range(8):
                            bQ, B_Q = qkv_rot.next()

                            def f_Q(pe):
                                ins = None
                                for k in range(2):
                                    ins = pe.matmul(bQ[0:96, 0:T1], lhsT=wq[:, k, h * 96:(h + 1) * 96], rhs=cqn[:, k, :], start=(k == 0), stop=(k == 1))
                                for k in range(2):
                                    ins = pe.matmul(bQ[0:96, T1:2 * T1], lhsT=wqr[:, k, h, :], rhs=cqn[:, k, :], start=(k == 0), stop=(k == 1))
                                return ins
                            S.op("pe", f_Q, reads=[B_wq, B_wqr, B_cqn], writes=[B_Q])
                            S.op("dve", lambda v: v.tensor_copy(out=qT[par][0:64, h, :], in_=bQ[0:64, 0:T1]),
                                 reads=[B_Q], writes=[B_qT[par][h]])
                            yield
                            S.op("dve", lambda v: v.tensor_tensor(out=t1[R, :], in0=bQ[R, 0:T1], in1=cos_t[R, :], op=ALU.mult),
                                 reads=[B_Q, B_cs], writes=[B_t1])
                            S.op("dve", lambda v: v.tensor_tensor(out=t2[R, :], in0=bQ[R, T1:2 * T1], in1=sin_t[R, :], op=ALU.mult),
                                 reads=[B_Q, B_cs], writes=[B_t2])
                            S.op("dve", lambda v: v.tensor_tensor(out=qT[par][R, h, :], in0=t1[R, :], in1=t2[R, :], op=ALU.add),
                                 reads=[B_t1, B_t2], writes=[B_qT[par][h]])
                            yield

                        for hp in range(4):
                            bK, B_K = qkv_rot.next()

                            def f_K(pe):
                                pe.matmul(bK[0:64, 0:T1], lhsT=wkv[:, (2 * hp) * 128:(2 * hp) * 128 + 64], rhs=cqn[:, 2, :], start=True, stop=True)
                                return pe.matmul(bK[0:64, T1:2 * T1], lhsT=wkv[:, (2 * hp + 1) * 128:(2 * hp + 1) * 128 + 64], rhs=cqn[:, 2, :],
                                                 start=True, stop=True)
                            S.op("pe", f_K, reads=[B_wkv, B_cqn], writes=[B_K])

                            def f_Kc(v):
                                v.tensor_copy(out=kT[0:64, 2 * hp, t0:t0 + T1], in_=bK[0:64, 0:T1])
                                return v.tensor_copy(out=kT[0:64, 2 * hp + 1, t0:t0 + T1], in_=bK[0:64, T1:2 * T1])
                            S.op("dve", f_Kc, reads=[B_K], writes=[B_kTn[jt][hp]])
                            yield
                        wkv_v = wkv[:].rearrange("p (h d) -> p h d", d=128)[:, :, 64:128]
                        for s in range(2):
                            bV, B_V = qkv_rot.next()
                            bV3 = bV[:, :].rearrange("p (h d) -> p h d", d=64)
                            S.op("pe", lambda pe: pe.matmul(bV3, lhsT=cqn[:, 2, s * 128:(s + 1) * 128], rhs=wkv_v, start=True, stop=True),
                                 reads=[B_wkv, B_cqn], writes=[B_V])
                            S.op("dve", lambda v: v.tensor_copy(out=vaug[:, 2 * jt + s, :, 0:64], in_=bV3),
                                 reads=[B_V, B_vinit], writes=[B_v[jt][s]])
                            yield


                    def conv_gen():
                        for c in range(4):
                            bX, B_X = F0, B_F0

                            def f_X(pe):
                                inproj(pe, 928 + c * 128, 128, bX[:, 0:T1])
                                return inproj(pe, 1440 + c * 128, 128, bX[:, T1:2 * T1])
                            S.op("pe", f_X, reads=[B_win, B_hT], writes=[B_X])
                            bY, B_Y = F1, B_F1a
                            S.op("pe", lambda pe: inproj(pe, 416 + c * 128, 128, bY[:, 0:T1]), reads=[B_win, B_hT], writes=[B_Y])
                            yield
                            S.op("dve", lambda v: v.tensor_copy(out=gcs[:], in_=bX[:, 0:T1]), reads=[B_X], writes=[B_gcs])
                            S.op("dve", lambda v: v.tensor_tensor(out=ub[:, c, 2:T1 + 2], in0=bX[:, T1:2 * T1], in1=gcs[:], op=ALU.mult),
                                 reads=[B_X, B_gcs], writes=[B_u[c]])
                            yield
                            cw = C_CW + c * 3
                            S.op("dve", lambda v: v.tensor_scalar(out=cacc[:], in0=ub[:, c, 2:T1 + 2], scalar1=pp[:, cw + 2:cw + 3],
                                                                  scalar2=pp[:, C_CB + c:C_CB + c + 1], op0=ALU.mult, op1=ALU.add),
                                 reads=[B_u[c], B_pp], writes=[B_cacc])
                            S.op("dve", lambda v: v.scalar_tensor_tensor(out=cacc[:], in0=ub[:, c, 1:T1 + 1], scalar=pp[:, cw + 1:cw + 2],
                                                                         in1=cacc[:], op0=ALU.mult, op1=ALU.add),
                                 reads=[B_u[c], B_cacc, B_pp], writes=[B_cacc])
                            yield
                            S.op("dve", lambda v: v.scalar_tensor_tensor(out=cacc[:], in0=ub[:, c, 0:T1], scalar=pp[:, cw:cw + 1],
                                                                         in1=cacc[:], op0=ALU.mult, op1=ALU.add),
                                 reads=[B_u[c], B_cacc, B_pp], writes=[B_cacc])
                            S.op("dve", lambda v: v.tensor_copy(out=ub[:, c, 0:2], in_=ub[:, c, T1:T1 + 2]), reads=[B_u[c]], writes=[B_u[c]])
                            yield
                            S.op("dve", lambda v: v.tensor_tensor(out=yc[:], in0=bY[:, 0:T1], in1=cacc[:], op=ALU.mult),
                                 reads=[B_Y, B_cacc], writes=[B_yc])
                            S.op("act", lambda a: a.activation(out=sqc[:], in_=yc[:], func=AF.Square), reads=[B_yc], writes=[B_sqc])
                            yield
                            bZ, B_Z = F1, B_F1b
                            S.op("pe", lambda pe: pe.matmul(bZ[:, T1:2 * T1], lhsT=bd[:], rhs=sqc[:], start=True, stop=True),
                                 reads=[B_sqc, B_const], writes=[B_Z])
                            for _ in range(4):
                                yield
                            S.op("act", lambda a: a.activation(out=rstdc[:], in_=bZ[:, T1:2 * T1], func=AF.Ln, bias=epst[:, 1:2], scale=1.0 / 64),
                                 reads=[B_Z, B_eps], writes=[B_rstdc])
                            yield
                            S.op("act", lambda a: a.activation(out=rstdc[:], in_=rstdc[:], func=AF.Exp, scale=-0.5), reads=[B_rstdc], writes=[B_rstdc])
                            S.op("dve", lambda v: v.scalar_tensor_tensor(out=ycT[par][:, c, :], in0=yc[:], scalar=pp[:, C_ONGC + c:C_ONGC + c + 1],
                                                                         in1=rstdc[:], op0=ALU.mult, op1=ALU.mult),
                                 reads=[B_yc, B_rstdc, B_pp], writes=[B_ycT[par]])
                            yield


                    yield from qkv_gen()
                    yield from conv_gen()

                def back(tile):
                    b = tile // 8
                    jt = tile % 8
                    par = tile % 2
                    t0 = jt * T1
                    g0 = b * SEQ + t0
                    xs = xs_of.pop(tile)
                    if jt == 0:
                        S.dma("sp", gbc[:], gsc_d[b * 2:b * 2 + 1, :].partition_broadcast(128), reads=[B_gsc], writes=[B_gbc])
                    npair = jt + 1
                    steps = [(h, pr) for h in range(8) for pr in range(npair)]

                    def emit_qk(h, pr):
                        diag = (pr == jt)
                        bS, B_S = mmB.next()
                        kt0, kt1 = 2 * pr, 2 * pr + 1
                        c1 = 128 if diag else 0
                        rd = [B_qT[par][h], B_kTn[pr][h // 2], B_kTr[pr]]

                        def f(pe):
                            ins = None
                            for _rep in range(QK_REP):
                                pe.matmul(bS[:, 0:T1], lhsT=kT[0:96, h, kt0 * 128:(kt0 + 1) * 128], rhs=qT[par][0:96, h, 0:T1], start=True, stop=True)
                                ins = pe.matmul(bS[:, T1 + c1:2 * T1], lhsT=kT[0:96, h, kt1 * 128:(kt1 + 1) * 128], rhs=qT[par][0:96, h, c1:T1],
                                                start=True, stop=True)
                            return ins
                        S.op("pe", f, reads=rd, writes=[B_S])
                        return bS, B_S, diag

                    def emit_exp_pv(h, pr, bS, B_S, diag):
                        kt0, kt1 = 2 * pr, 2 * pr + 1
                        if diag:
                            pT, B_p = pTd, B_pTd

                            def f_e(a):
                                a.activation(out=pT[0:64, 0:T1], in_=bS[0:64, 0:T1], func=AF.Exp, scale=SCALE)
                                a.activation(out=pT[0:64, T1 + 128:2 * T1], in_=bS[0:64, T1 + 128:2 * T1], func=AF.Exp, scale=SCALE)
                                a.activation(out=pT[64:128, 64:T1], in_=bS[64:128, 64:T1], func=AF.Exp, scale=SCALE)
                                return a.activation(out=pT[64:128, T1 + 192:2 * T1], in_=bS[64:128, T1 + 192:2 * T1], func=AF.Exp, scale=SCALE)
                        else:
                            pT, B_p = p_rot.next()

                            def f_e(a):
                                return a.activation(out=pT[:, :], in_=bS[:, :], func=AF.Exp, scale=SCALE)
                        S.op("act", f_e, reads=[B_S], writes=[B_p])
                        co = 0
                        accb = accbs[h % 2]
                        c1 = 128 if diag else 0
                        last = (pr == npair - 1)

                        def f_pv(pe):
                            pe.matmul(accb[0:65, co:co + T1], lhsT=vaug[:, kt0, h, :], rhs=pT[:, 0:T1], start=(pr == 0), stop=False)
                            return pe.matmul(accb[0:65, co + c1:co + T1], lhsT=vaug[:, kt1, h, :], rhs=pT[:, T1 + c1:2 * T1], start=False, stop=last)
                        S.op("pe", f_pv, reads=[B_p, B_v[pr][0], B_v[pr][1]], writes=[B_acc[h % 2]])

                    def finalize_a(h):
                        i2 = h % 2
                        co = 0
                        accb = accbs[i2]
                        S.op("dve", lambda v: v.tensor_copy(out=osb[i2][0:65, :], in_=accb[0:65, co:co + T1]), reads=[B_acc[i2]], writes=[B_osb[i2]])
                        S.op("dve", lambda v: v.tensor_tensor(out=osq[i2][0:65, :], in0=osb[i2][0:65, :], in1=osb[i2][0:65, :], op=ALU.mult),
                             reads=[B_osb[i2]], writes=[B_osq[i2]])

                    def finalize_b(h, part=0):
                        i2 = h % 2
                        co = T1
                        accb = accbs[i2]
                        if part in (0, 1):
                            S.op("pe", lambda pe: pe.matmul(accb[0:64, co:co + T1], lhsT=wn[0:65, :], rhs=osq[i2][0:65, :], start=True, stop=True),
                                 reads=[B_osq[i2], B_const], writes=[B_acc[i2]])
                        if part == 1:
                            return
                        S.op("act", lambda a: a.activation(out=orstd[0:64, :], in_=accb[0:64, co:co + T1], func=AF.Ln, scale=1.0 / 64), reads=[B_acc[i2]], writes=[B_orstd])
                        S.op("act", lambda a: a.activation(out=orstd[0:64, :], in_=orstd[0:64, :], func=AF.Exp, scale=-0.5), reads=[B_orstd], writes=[B_orstd])
                        if h % 2 == 0:
                            S.op("dve", lambda v: v.scalar_tensor_tensor(out=yaT[0:64, h // 2, :], in0=osb[i2][0:64, :], scalar=pp[0:64, C_ONGA + h:C_ONGA + h + 1],
                                                                         in1=orstd[0:64, :], op0=ALU.mult, op1=ALU.mult),
                                 reads=[B_osb[i2], B_orstd, B_pp], writes=[B_yaT[h]])
                        else:
                            k2 = (h // 2) % 2
                            S.op("dve", lambda v: v.scalar_tensor_tensor(out=ystg[k2][0:64, :], in0=osb[i2][0:64, :], scalar=pp[0:64, C_ONGA + h:C_ONGA + h + 1],
                                                                         in1=orstd[0:64, :], op0=ALU.mult, op1=ALU.mult),
                                 reads=[B_osb[i2], B_orstd, B_pp], writes=[B_ystg[k2]])
                            S.dma("sp", yaT[64:128, h // 2, :], ystg[k2][0:64, :], reads=[B_ystg[k2]], writes=[B_yaT[h]])

                    LA = 2
                    qq = [emit_qk(*steps[i0]) for i0 in range(min(LA + 1, len(steps)))]
                    nq = len(qq)
                    fa_pend = []
                    fb_pend = []

                    fc_pend = []

                    def flush(upto_step, force_head=None):
                        did = False
                        for it in list(fc_pend):
                            if it[0] <= upto_step or it[1] == force_head:
                                finalize_b(it[1], part=2); fc_pend.remove(it); did = True
                        for it in list(fb_pend):
                            if it[0] <= upto_step or it[1] == force_head:
                                fb_pend.remove(it); did = True
                                if it[1] == force_head:
                                    finalize_b(it[1])
                                else:
                                    finalize_b(it[1], part=1)
                                    fc_pend.append([upto_step + 2, it[1]])
                        for it in list(fa_pend):
                            if it[0] <= upto_step or it[1] == force_head:
                                finalize_a(it[1]); fa_pend.remove(it); did = True
                                if it[1] == force_head:
                                    finalize_b(it[1])
                                else:
                                    fb_pend.append([upto_step + 1, it[1]])
                        return did

                    for i, (h, pr) in enumerate(steps):
                        if pr == 0 and h >= 2:
                            flush(-1, force_head=h - 2)
                        cur = qq.pop(0)
                        emit_exp_pv(h, pr, *cur)
                        if nq < len(steps):
                            qq.append(emit_qk(*steps[nq]))
                            nq += 1
                        yield
                        if flush(i):
                            yield
                        if pr == npair - 1:
                            fa_pend.append([i + 1, h])
                    for it in list(fa_pend):
                        finalize_a(it[1]); fa_pend.remove(it); fb_pend.append([0, it[1]])
                    yield
                    for it in list(fb_pend):
                        finalize_b(it[1], part=1); fb_pend.remove(it); fc_pend.append([0, it[1]])
                    yield
                    for it in list(fc_pend):
                        finalize_b(it[1], part=2); fc_pend.remove(it)
                    yield

                    for s in range(2):
                        xt, B_x = xs[s]
                        x1t, B_x1 = x1_rot.next()
                        for n in range(2):
                            bO, B_O = mmB.next()

                            def f_O(pe):
                                for i in range(4):
                                    pe.matmul(bO[:, :], lhsT=yaT[:, i, s * 128:(s + 1) * 128], rhs=woa[:, i, n * 512:(n + 1) * 512],
                                              start=(i == 0), stop=False)
                                ins = None
                                for c in range(4):
                                    ins = pe.matmul(bO[:, :], lhsT=ycT[par][:, c, s * 128:(s + 1) * 128], rhs=woc[:, c, n * 512:(n + 1) * 512],
                                                    start=False, stop=(c == 3))
                                return ins
                            S.op("pe", f_O, reads=B_yaT + [B_ycT[par], B_woa, B_woc], writes=[B_O])
                            S.op("dve", lambda v: v.tensor_tensor(out=x1t[:, n * 512:(n + 1) * 512], in0=bO[:, :],
                                                                  in1=gbc[:, n * 512:(n + 1) * 512], op=ALU.mult),
                                 reads=[B_O, B_gbc], writes=[B_x1])
                            yield
                        S.op("dve", lambda v: v.scalar_tensor_tensor(out=x1t[:], in0=xt[:], scalar=ALPHA, in1=x1t[:], op0=ALU.mult, op1=ALU.add),
                             reads=[B_x, B_x1], writes=[B_x1])
                        yield
                        hh = ln_stats_a(x1t, B_x1, st_rot_b)
                        rs, B_rs, mv, B_mv = hh
                        for _ in range(5):
                            yield
                        ln_stats_b(hh)
                        yield
                        S.op("dve", lambda v: v.scalar_tensor_tensor(out=x1t[:], in0=x1t[:], scalar=mv[:, 0:1], in1=lnb[:, 0, :], op0=ALU.subtract, op1=ALU.mult),
                             reads=[B_x1, B_mv, B_lnb], writes=[B_x1])
                        yield
                        S.op("dve", lambda v: v.scalar_tensor_tensor(out=x1t[:], in0=x1t[:], scalar=rs[:, 0:1], in1=lnb[:, 1, :], op0=ALU.mult, op1=ALU.add),
                             reads=[B_x1, B_rs, B_lnb], writes=[B_x1])
                        gs = g0 + s * 128
                        S.dma("sp", x1s_d[gs:gs + 128, :], x1t[:], reads=[B_x1])
                        yield
                        hh2 = ln_stats_a(x1t, B_x1, st_rot_b)
                        rsb, B_rsb = hh2[0], hh2[1]
                        for _ in range(5):
                            yield
                        ln_stats_b(hh2, want_nb=True)
                        S.op("dve", lambda v: v.tensor_copy(out=rs2[:, s, :], in_=rsb[:]), reads=[B_rsb], writes=[B_rs2])
                        yield
                    S.dma("sp", st2_d[2 * tile:2 * tile + 2, :, :].rearrange("s p c -> p s c"), rs2[:], reads=[B_rs2])
                    yield

                def run_interleaved(gens):
                    active = [g for g in gens if g is not None]
                    while active:
                        for g in list(active):
                            try:
                                next(g)
                            except StopIteration:
                                active.remove(g)

                for b in range(NBC):
                    run_interleaved([front(b * 8)])
                    for jt in range(8):
                        tile = b * 8 + jt
                        run_interleaved([back(tile), front(tile + 1) if jt < 7 else None])
                S.barrier()

        with ExitStack() as es2:
            wdn = sbt(es2, "wdn", [128, NPAIR, 1024], BF16); B_wdn = Buf("wdn")
            wups = [sbt(es2, f"wup{i}", [128, 8, 2, 128], BF16) for i in range(6)]
            B_wups = [Buf(f"wup{i}") for i in range(6)]
            tmp = sbt(es2, "tmp", [128, 1024]); B_tmp = Buf("tmp")
            x1ts = [sbt(es2, f"x1t{i}", [128, 1024]) for i in range(8)]
            B_x1ts = [Buf(f"x1t{i}") for i in range(8)]
            h2T = [sbt(es2, f"h2T{i}", [128, 8, T2], BF16) for i in range(2)]; B_h2T = [Buf(f"h2T{i}") for i in range(2)]
            accg = [sbt(es2, f"accg{i}", [128, T2]) for i in range(2)]
            accv = [sbt(es2, f"accv{i}", [128, T2]) for i in range(2)]
            B_accg = [Buf(f"accg{i}") for i in range(2)]
            B_accv = [Buf(f"accv{i}") for i in range(2)]
            actT = [sbt(es2, f"actT{i}", [128, NPAIR, T2], BF16) for i in range(2)]
            B_actT = [[Buf(f"actT{i}_{p}") for p in range(NPAIR)] for i in range(2)]
            halo = sbt(es2, "halo", [128, 44, 2]); B_halo = [Buf(f"halo{c}") for c in range(44)]
            hcor = sbt(es2, "hcor", [128, 44, 2]); B_hcor = Buf("hcor")
            htmp = sbt(es2, "htmp", [128, 44]); B_htmp = Buf("htmp")
            st2t = [sbt(es2, f"st2t{i}", [128, 4, 2]) for i in range(2)]; B_st2 = [Buf(f"st2t{i}") for i in range(2)]
            st4 = sbt(es2, "st4", [128, 4, 2, 6]); mv4 = sbt(es2, "mv4", [128, 4, 2]); rs4 = sbt(es2, "rs4", [128, 4, 2])
            B_st4 = [Buf(f"st4_{i}") for i in range(4)]
            B_mv4 = Buf("mv4"); B_rs4 = Buf("rs4")
            mmM = Rot([(mmb[i], B_mm[i]) for i in range(4)])
            mmK = Rot([(mmb[4], B_mm[4]), (mmb[5], B_mm[5]), (mmb[6], B_mm[6])])

            S.dma("sp", lnb[:, 0, :], lnp_d[2:3, :].partition_broadcast(128), writes=[B_lnb])
            S.dma("sp", lnb[:, 1, :], lnp_d[3:4, :].partition_broadcast(128), writes=[B_lnb], group=True)
            fcw3 = pp[:, C_FCW:C_FCW + 132].rearrange("p (c k) -> p c k", k=3)

            wu_i = [0]

            def load_wup(p):
                sl = wu_i[0] % 6
                wu_i[0] += 1
                S.dma("pool", wups[sl][:].rearrange("p k g n -> p (k g n)"), wupb_d[p], reads=[B_wupb], writes=[B_wups[sl]])
                return sl
            PF = 5

            xn2 = [sbt(es2, f"xn2_{i}", [128, 1024], BF16) for i in range(4)]
            B_xn2 = [Buf(f"xn2_{i}") for i in range(4)]

            def front2(tile):
                g0 = tile * T2
                b = g0 // SEQ
                par = tile % 2
                S.dma("sp", st2t[par][:], st2_d[4 * tile:4 * tile + 4, :, :].rearrange("s p c -> p s c"), writes=[B_st2[par]])
                for s in range(4):
                    xi = par * 4 + s
                    S.dma("sp", x1ts[xi][:], x1s_d[g0 + s * 128:g0 + (s + 1) * 128, :], writes=[B_x1ts[xi]])
                yield
                for s in range(4):
                    xi = par * 4 + s
                    S.op("act", lambda a: a.activation(out=xn2[s][:], in_=x1ts[xi][:], func=AF.Identity, bias=st2t[par][:, s, 1:2], scale=st2t[par][:, s, 0:1]),
                         reads=[B_x1ts[xi], B_st2[par]], writes=[B_xn2[s]])
                    yield
                yield
                yield
                for s in range(4):
                    xn, B_xn = xn2[s], B_xn2[s]
                    tp, B_tp = tp_rot.next()

                    def f_tp(pe):
                        ins = None
                        for k in range(8):
                            ins = pe.transpose(out=tp[:, k * 128:(k + 1) * 128], in_=xn[:, k * 128:(k + 1) * 128], identity=identb[:])
                        return ins
                    S.op("pe", f_tp, reads=[B_xn, B_idb], writes=[B_tp])
                    yield

                    def f_ev(a):
                        ins = None
                        for k in range(8):
                            ins = a.activation(out=h2T[par][:, k, s * 128:(s + 1) * 128], in_=tp[:, k * 128:(k + 1) * 128], func=AF.Identity,
                                               bias=modT[:, 24 + k, b:b + 1], scale=modT[:, 32 + k, b:b + 1])
                        return ins
                    S.op("act", f_ev, reads=[B_tp, B_mod], writes=[B_h2T[par]])
                    yield
                    yield

            def main2(tile):
                g0 = tile * T2
                b = g0 // SEQ
                par = tile % 2
                if (g0 % SEQ) == 0:
                    S.op("dve", lambda v: v.memset(hcor[:], 0.0), writes=[B_hcor])
                slots = [load_wup(p) for p in range(PF)]
                for p in range(NPAIR):
                    if p + PF < NPAIR:
                        slots.append(load_wup(p + PF))
                    if tile == 0 and p < 11:
                        S.dma("pool", wdn[:, 2 * p:2 * p + 2, :], wdnb_d[:, 2 * p:2 * p + 2, :], reads=[B_wdnb], writes=[B_wdn], group=True)
                    sl = slots[p]
                    i2 = p % 2
                    bG, B_G = mmM.next()
                    bU, B_U = mmM.next()

                    def f_up(pe, g, bank):
                        ins = None
                        for k in range(8):
                            ins = pe.matmul(bank[:, :], lhsT=wups[sl][:, k, g, :], rhs=h2T[par][:, k, :], start=(k == 0), stop=(k == 7))
                        return ins
                    S.op("pe", lambda pe: f_up(pe, 0, bG), reads=[B_wups[sl], B_h2T[par]], writes=[B_G])
                    S.op("pe", lambda pe: f_up(pe, 1, bU), reads=[B_wups[sl], B_h2T[par]], writes=[B_U])
                    yield
                    for (bank, B_bank, acc, B_acc, c) in ((bG, B_G, accg[i2], B_accg[i2], p), (bU, B_U, accv[i2], B_accv[i2], NPAIR + p)):
                        S.op("act", lambda a: a.activation(out=acc[:], in_=bank[:, :], func=AF.Identity,
                                                           bias=pp[:, C_FCB + c:C_FCB + c + 1], scale=fcw3[:, c, 2:3]),
                             reads=[B_bank, B_pp], writes=[B_acc])
                        S.op("act", lambda a: a.activation(out=halo[:, c, :], in_=bank[:, T2 - 2:T2], func=AF.Copy),
                             reads=[B_bank], writes=[B_halo[c]])
                        yield
                        S.op("dve", lambda v: v.scalar_tensor_tensor(out=acc[:, 1:T2], in0=bank[:, 0:T2 - 1], scalar=fcw3[:, c, 1:2],
                                                                     in1=acc[:, 1:T2], op0=ALU.mult, op1=ALU.add),
                             reads=[B_bank, B_acc, B_pp], writes=[B_acc])
                        yield
                        S.op("dve", lambda v: v.scalar_tensor_tensor(out=acc[:, 2:T2], in0=bank[:, 0:T2 - 2], scalar=fcw3[:, c, 0:1],
                                                                     in1=acc[:, 2:T2], op0=ALU.mult, op1=ALU.add),
                             reads=[B_bank, B_acc, B_pp], writes=[B_acc])
                        yield
                        S.op("dve", lambda v: v.tensor_tensor(out=acc[:, 0:2], in0=acc[:, 0:2], in1=hcor[:, c, :], op=ALU.add),
                             reads=[B_acc, B_hcor], writes=[B_acc])
                        yield
                    S.op("act", lambda a: a.activation(out=accg[i2][:], in_=accg[i2][:], func=AF.Silu), reads=[B_accg[i2]], writes=[B_accg[i2]])
                    yield
                    S.op("dve", lambda v: v.tensor_tensor(out=actT[par][:, p, :], in0=accg[i2][:], in1=accv[i2][:], op=ALU.mult),
                         reads=[B_accg[i2], B_accv[i2]], writes=[B_actT[par][p]])
                    yield

                S.op("dve", lambda v: v.tensor_tensor(out=htmp[:], in0=halo[:, :, 0], in1=fcw3[:, :, 0], op=ALU.mult),
                     reads=B_halo + [B_pp], writes=[B_htmp])
                S.op("dve", lambda v: v.tensor_tensor(out=hcor[:, :, 1], in0=halo[:, :, 1], in1=fcw3[:, :, 0], op=ALU.mult),
                     reads=B_halo + [B_pp], writes=[B_hcor])
                S.op("dve", lambda v: v.tensor_tensor(out=hcor[:, :, 0], in0=halo[:, :, 1], in1=fcw3[:, :, 1], op=ALU.mult),
                     reads=B_halo + [B_pp, B_hcor], writes=[B_hcor])
                S.op("dve", lambda v: v.tensor_tensor(out=hcor[:, :, 0], in0=hcor[:, :, 0], in1=htmp[:], op=ALU.add),
                     reads=[B_htmp, B_hcor], writes=[B_hcor])
                yield

            def back2(tile):
                g0 = tile * T2
                b = g0 // SEQ
                par = tile % 2
                if (g0 % SEQ) == 0:
                    S.dma("sp", gbc[:], gsc_d[b * 2 + 1:b * 2 + 2, :].partition_broadcast(128), writes=[B_gbc])
                for s in range(4):
                    xi = par * 4 + s
                    for n in range(2):
                        bO, B_O = mmK.next()

                        def f_dn(pe):
                            ins = None
                            for p in range(NPAIR):
                                ins = pe.matmul(bO[:, :], lhsT=actT[par][:, p, s * 128:(s + 1) * 128], rhs=wdn[:, p, n * 512:(n + 1) * 512],
                                                start=(p == 0), stop=(p == NPAIR - 1))
                            return ins
                        S.op("pe", f_dn, reads=B_actT[par] + [B_wdn], writes=[B_O])
                        S.op("dve", lambda v: v.tensor_tensor(out=tmp[:, n * 512:(n + 1) * 512], in0=bO[:, :], in1=gbc[:, n * 512:(n + 1) * 512], op=ALU.mult),
                             reads=[B_O, B_gbc], writes=[B_tmp])
                        yield
                    S.op("dve", lambda v: v.scalar_tensor_tensor(out=x1ts[xi][:], in0=x1ts[xi][:], scalar=ALPHA, in1=tmp[:], op0=ALU.mult, op1=ALU.add),
                         reads=[B_x1ts[xi], B_tmp], writes=[B_x1ts[xi]])
                    yield

                    def f_st(v):
                        v.bn_stats(out=st4[:, s, 0, :], in_=x1ts[xi][:, 0:512])
                        return v.bn_stats(out=st4[:, s, 1, :], in_=x1ts[xi][:, 512:1024])
                    S.op("dve", f_st, reads=[B_x1ts[xi]], writes=[B_st4[s]])
                    S.op("dve", lambda v: v.bn_aggr(out=mv4[:, s, :], in_=st4[:, s, :, :]), reads=[B_st4[s]], writes=[B_mv4])
                    yield
                S.op("act", lambda a: a.activation(out=rs4[:, :, 0], in_=mv4[:, :, 1], func=AF.Sqrt, bias=epsP2[:, 0:1], scale=1.0), reads=[B_mv4, B_epsP2], writes=[B_rs4])
                S.op("dve", lambda v: v.reciprocal(out=rs4[:, :, 0], in_=rs4[:, :, 0]), reads=[B_rs4], writes=[B_rs4])
                yield
                for s in range(4):
                    xi = par * 4 + s
                    S.op("dve", lambda v: v.scalar_tensor_tensor(out=x1ts[xi][:], in0=x1ts[xi][:], scalar=mv4[:, s, 0:1], in1=lnb[:, 0, :], op0=ALU.subtract, op1=ALU.mult),
                         reads=[B_x1ts[xi], B_mv4, B_lnb], writes=[B_x1ts[xi]])
                    yield
                    S.op("dve", lambda v: v.scalar_tensor_tensor(out=x1ts[xi][:], in0=x1ts[xi][:], scalar=rs4[:, s, 0:1], in1=lnb[:, 1, :], op0=ALU.mult, op1=ALU.add),
                         reads=[B_x1ts[xi], B_rs4, B_lnb], writes=[B_x1ts[xi]])
                    S.dma("sp", out_d[g0 + s * 128:g0 + (s + 1) * 128, :], x1ts[xi][:], reads=[B_x1ts[xi]])
                    yield

            epsP2 = sbt(es2, "epsP2", [128, 1]); B_epsP2 = Buf("epsP2")
            S.op("dve", lambda v: v.memset(epsP2[:], LN_EPS), writes=[B_epsP2])

            def run_weighted(gens):
                active = [[g, w] for g, w in gens if g is not None]
                while active:
                    for item in list(active):
                        g, w = item
                        for _ in range(w):
                            try:
                                next(g)
                            except StopIteration:
                                active.remove(item)
                                break

            run_weighted([(front2(0), 1)])
            for tile in range(NT2):
                def chain(tile=tile):
                    if tile >= 1:
                        yield from back2(tile - 1)
                    if tile + 1 < NT2:
                        yield from front2(tile + 1)
                run_weighted([(main2(tile), 4), (chain(), 1)])
            run_weighted([(back2(NT2 - 1), 1)])
            S.barrier()
    return nc


def LN_EPS_P2():
    return LN_EPS


def _bf(x):
    return np.ascontiguousarray(x, dtype=np.float32)


def prep_inputs(inp):
    x = np.asarray(inp["x"], np.float32)
    c = np.asarray(inp["c"], np.float32)
    pos = np.asarray(inp["positions"], np.int32)
    w_ada = _bf(np.asarray(inp["w_ada"])[0].reshape(8, 128, 6144).transpose(1, 0, 2))
    w_in = _bf(np.asarray(inp["w_in"])[0].reshape(8, 128, 1952).transpose(1, 0, 2))
    wq = _bf(np.asarray(inp["w_q_up"])[0].reshape(2, 128, 768).transpose(1, 0, 2))
    wkv = _bf(np.asarray(inp["w_kv_up"])[0])
    w_out = np.asarray(inp["w_out"])[0]
    woa = _bf(w_out[:512].reshape(4, 128, 1024).transpose(1, 0, 2))
    woc = _bf(w_out[512:].reshape(4, 128, 1024).transpose(1, 0, 2))
    w_up = np.asarray(inp["w_up"])[0]
    wup = _bf(w_up.reshape(8, 128, 2, NPAIR, 128).transpose(3, 1, 0, 2, 4))
    wdn = _bf(np.asarray(inp["w_down"])[0].reshape(NPAIR, 128, 1024).transpose(1, 0, 2))
    lnp = _bf(np.stack([np.asarray(inp["ln1_g"])[0], np.asarray(inp["ln1_b"])[0], np.asarray(inp["ln2_g"])[0], np.asarray(inp["ln2_b"])[0]]))
    ident = np.eye(128, dtype=np.float32)
    inv_freq = (np.float64(10000.0) ** (-(np.arange(0, 32, 2, dtype=np.float64) / 32.0))).astype(np.float32)

    ppb = np.zeros((128, NPP), np.float32)
    ppb[:, C_BADA:C_BADA + 48] = np.asarray(inp["b_ada"])[0].reshape(48, 128).T
    ppb[:, C_QG:C_QG + 2] = np.asarray(inp["q_norm_g"])[0].reshape(2, 128).T
    ppb[:, C_KVG] = np.asarray(inp["kv_norm_g"])[0]
    ong = np.asarray(inp["out_norm_g"])[0]
    ppb[0:64, C_ONGA:C_ONGA + 8] = ong[:512].reshape(8, 64).T
    ppb[:, C_ONGC:C_ONGC + 4] = ong[512:].reshape(4, 128).T
    ppb[:, C_CW:C_CW + 12] = np.asarray(inp["conv_w"])[0].reshape(3, 4, 128).transpose(2, 1, 0).reshape(128, 12)
    ppb[:, C_CB:C_CB + 4] = np.asarray(inp["conv_b"])[0].reshape(4, 128).T
    ppb[:, C_FCW:C_FCW + 132] = np.asarray(inp["ffn_conv_w"])[0].reshape(3, 44, 128).transpose(2, 1, 0).reshape(128, 132)
    ppb[:, C_FCB:C_FCB + 44] = np.asarray(inp["ffn_conv_b"])[0].reshape(44, 128).T
    ppb[:, C_INVF] = np.tile(inv_freq, 8)
    maps = []
    for core in range(8):
        pc = ppb.copy()
        cc = c[2 * core:2 * core + 2]
        pc[:, C_CT:C_CT + 16] = cc.reshape(2, 8, 128).transpose(2, 1, 0).reshape(128, 16)
        maps.append(dict(
            x=np.ascontiguousarray(x[2 * core:2 * core + 2].reshape(TOK, D)),
            pos=np.ascontiguousarray(pos[2 * core:2 * core + 2].reshape(1, TOK)),
            pp=pc, w_ada=w_ada, w_in=w_in, wq=wq, wkv=wkv, woa=woa, woc=woc, wup=wup, wdn=wdn, lnp=lnp, ident=ident))
    return maps


_NC_CACHE = {}


def kernel(**inputs):
    maps = prep_inputs(inputs)
    if "nc" not in _NC_CACHE:
        _NC_CACHE["nc"] = build()
    nc = _NC_CACHE["nc"]
    res = run_bass_kernel_spmd(nc, maps, core_ids=list(range(8)))
    out = np.stack([r["out"].reshape(NBC, SEQ, D) for r in res.results], axis=0).reshape(16, SEQ, D)
    return np.ascontiguousarray(out.astype(np.float32))
```

```python
import numpy as np
from contextlib import ExitStack
import concourse.bass as bass
import concourse.mybir as mybir
from concourse.bass_utils import run_bass_kernel_spmd

F32 = mybir.dt.float32
BF16 = mybir.dt.bfloat16
I32 = mybir.dt.int32
AF = mybir.ActivationFunctionType
ALU = mybir.AluOpType

D = 1024
SEQ = 2048
NBC = 2
TOK = NBC * SEQ
T1 = 256
NT1 = TOK // T1
T2 = 512
NT2 = TOK // T2
NH = 8
DFF = 2816
NPAIR = DFF // 128
ALPHA = 2.0 ** 0.25
LN_EPS = 1e-5
RMS_EPS = 1e-6
SCALE = 96.0 ** -0.5
QK_REP = 1

C_BADA = 0
C_QG = 48
C_KVG = 50
C_ONGA = 51
C_ONGC = 59
C_CW = 63
C_CB = 75
C_FCW = 79
C_FCB = 211
C_INVF = 255
C_CT = 256
NPP = 272

TWO_PI = float(2 * np.pi)
CW1 = 6.28125
CW2 = TWO_PI - CW1


class Buf:
    __slots__ = ("name", "w", "r", "dsem", "dval")

    def __init__(self, name):
        self.name = name
        self.w = None
        self.r = {}
        self.dsem = None
        self.dval = 0


class Sched:
    def __init__(self, nc, es):
        self.nc = nc
        self.es = es
        self.eng = {"pe": nc.tensor, "act": nc.scalar, "dve": nc.vector, "pool": nc.gpsimd, "sp": nc.sync}
        self.sem = {k: es.enter_context(nc.semaphore("s_" + k)) for k in self.eng}
        self.cnt = {k: 0 for k in self.eng}
        self.seen = {k: {} for k in self.eng}
        self.dma_last = {}

    def _wait(self, e, tok):
        if tok is None:
            return
        sem, val = tok
        if self.seen[e].get(sem.num, 0) >= val:
            return
        self.eng[e].wait_ge(sem, val)
        self.seen[e][sem.num] = val

    def _deps(self, e, reads, writes):
        for b in reads:
            self._wait(e, b.w)
        for b in writes:
            self._wait(e, b.w)
            for t in list(b.r.values()):
                self._wait(e, t)

    @staticmethod
    def _commit(key, tok, reads, writes):
        for b in reads:
            b.r[key] = tok
        for b in writes:
            b.w = tok
            b.r = {}

    def op(self, e, fn, reads=(), writes=()):
        self._deps(e, reads, writes)
        ins = fn(self.eng[e])
        self.cnt[e] += 1
        ins.then_inc(self.sem[e], 1)
        tok = (self.sem[e], self.cnt[e])
        self._commit(e, tok, reads, writes)
        return tok

    def dma(self, q, out, in_, reads=(), writes=(), owner=None, group=False, **kw):
        ow = owner or (writes[0] if writes else reads[0])
        if ow.dsem is None:
            ow.dsem = self.es.enter_context(self.nc.semaphore("d_" + ow.name))
        for b in reads:
            self._wait(q, b.w)
        for b in writes:
            if not (group and b.w is not None and b.w[0] is ow.dsem):
                self._wait(q, b.w)
            for t in list(b.r.values()):
                self._wait(q, t)
        ow.dval += 16
        self.eng[q].dma_start(out=out, in_=in_, **kw).then_inc(ow.dsem, 16)
        tok = (ow.dsem, ow.dval)
        self._commit(("dma", ow.dsem.num), tok, reads, writes)
        self.dma_last[ow.dsem.num] = tok
        return tok

    def barrier(self):
        toks = [(self.sem[k], self.cnt[k]) for k in self.eng if self.cnt[k] > 0]
        toks += list(self.dma_last.values())
        for e in self.eng:
            for t in toks:
                if t[0] is self.sem[e]:
                    continue
                self._wait(e, t)


class Rot:
    def __init__(self, items):
        self.items = items
        self.i = 0

    def next(self):
        it = self.items[self.i % len(self.items)]
        self.i += 1
        return it


def build(dbg=False):
    nc = bass.Bass("TRN2", target_bir_lowering=False)

    def din(name, shape, dt=F32):
        return nc.dram_tensor(name, shape, dt, kind="ExternalInput").ap()

    x_d = din("x", [TOK, D])
    pos_d = din("pos", [1, TOK], I32)
    pp_d = din("pp", [128, NPP])
    wada_d = din("w_ada", [128, 8, 6144])
    win_d = din("w_in", [128, 8, 1952])
    wq_d = din("wq", [128, 2, 768])
    wkv_d = din("wkv", [128, 1024])
    woa_d = din("woa", [128, 4, 1024])
    woc_d = din("woc", [128, 4, 1024])
    wup_d = din("wup", [NPAIR, 128, 8, 2, 128])
    wdn_d = din("wdn", [128, NPAIR, 1024])
    lnp_d = din("lnp", [4, 1024])
    ident_d = din("ident", [128, 128])
    out_d = nc.dram_tensor("out", [TOK, D], F32, kind="ExternalOutput").ap()
    x1s_d = nc.dram_tensor("x1s", [TOK, D], F32, kind="ExternalOutput" if dbg else "Internal").ap()
    cs_d = nc.dram_tensor("cs", [2, 32, TOK], F32, kind="Internal").ap()
    st2_d = nc.dram_tensor("st2", [TOK // 128, 128, 2], F32, kind="Internal").ap()
    gsc_d = nc.dram_tensor("gsc", [4, 1024], F32, kind="Internal").ap()

    with ExitStack() as es:
        S = Sched(nc, es)

        def sbt(stack, name, shape, dt=F32):
            return stack.enter_context(nc.sbuf_tensor("sb_" + name, shape, dt))

        pp = sbt(es, "pp", [128, NPP]); B_pp = Buf("pp")
        identb = sbt(es, "identb", [128, 128], BF16); B_idb = Buf("idb")
        onesb = sbt(es, "onesb", [128, 128], BF16)
        wn = sbt(es, "wn", [128, 64], BF16)
        bd = sbt(es, "bd", [128, 128], BF16)
        B_const = Buf("const")
        modT = sbt(es, "modT", [128, 48, 2]); B_mod = Buf("mod")
        lnb = sbt(es, "lnb", [128, 2, 1024]); B_lnb = Buf("lnb")
        gbc = sbt(es, "gbc", [128, 1024]); B_gbc = Buf("gbc")
        xnb = [sbt(es, f"xnb{i}", [128, 1024], BF16) for i in range(2)]
        xn_rot = Rot([(xnb[i], Buf(f"xn{i}")) for i in range(2)])
        stt_ = [sbt(es, f"st{i}", [128, 2, 6]) for i in range(2)]
        mvt_ = [sbt(es, f"mv{i}", [128, 2]) for i in range(2)]
        rst_ = [sbt(es, f"rs{i}", [128, 2]) for i in range(2)]
        st_rot = Rot([(stt_[i], mvt_[i], rst_[i], Buf(f"st{i}"), Buf(f"mv{i}"), Buf(f"rs{i}")) for i in range(2)])

        tpb = [es.enter_context(nc.psum_tensor(f"tp{i}", [128, 1024], BF16)) for i in range(1)]
        tp_rot = Rot([(tpb[i], Buf(f"tp{i}")) for i in range(1)])
        mmb = [es.enter_context(nc.psum_tensor(f"mm{i}", [128, 512], F32)) for i in range(7)]
        B_mm = [Buf(f"mm{i}") for i in range(7)]

        S.dma("sp", pp[:], pp_d, writes=[B_pp])
        S.dma("pool", identb[:], ident_d, writes=[B_idb])

        def f_const(v):
            v.memset(onesb[:], 1.0)
            v.memset(wn[0:64, :], 1.0)
            v.memset(wn[64:128, :], 64.0 * RMS_EPS)
            return v.memset(bd[:], 0.0)
        S.op("dve", f_const, writes=[B_const])

        def f_const2(v):
            v.memset(bd[0:64, 0:64], 1.0)
            return v.memset(bd[64:128, 64:128], 1.0)
        S.op("dve", f_const2, writes=[B_const])

        with ExitStack() as es1:
            w_in = sbt(es1, "w_in", [128, 8, 1952], BF16); B_win = Buf("win")
            wkr = sbt(es1, "wkr", [128, 8, 2, 96], BF16); B_wkr = Buf("wkr")
            wq = sbt(es1, "wq", [128, 2, 768], BF16); B_wq = Buf("wq")
            wqr = sbt(es1, "wqr", [128, 2, 8, 96], BF16); B_wqr = Buf("wqr")
            wkv = sbt(es1, "wkv", [128, 1024], BF16); B_wkv = Buf("wkv")
            woa = sbt(es1, "woa", [128, 4, 1024], BF16); B_woa = Buf("woa")
            woc = sbt(es1, "woc", [128, 4, 1024], BF16); B_woc = Buf("woc")
            kT = sbt(es1, "kT", [128, 8, SEQ], BF16)
            vaug = sbt(es1, "vaug", [128, SEQ // 128, 8, 65], BF16)
            B_kTn = [[Buf(f"kTn{j}_{hp}") for hp in range(4)] for j in range(8)]
            B_kTr = [Buf(f"kTr{j}") for j in range(8)]
            B_v = [[Buf(f"v{j}_{s}") for s in range(2)] for j in range(8)]
            B_vinit = Buf("vinit")

            for k in range(0, 8, 2):
                S.dma("pool", w_in[:, k:k + 2, :], win_d[:, k:k + 2, :], writes=[B_win], group=True)
            S.dma("pool", wq[:], wq_d, writes=[B_wq])
            S.dma("pool", wkv[:], wkv_d, writes=[B_wkv])

            with ExitStack() as es0:
                cactb = sbt(es0, "cactb", [128, 16], BF16); B_cact = Buf("cact")
                wa = [sbt(es0, f"wa{i}", [128, 8, 512], BF16) for i in range(2)]
                B_wa = [Buf(f"wa{i}") for i in range(2)]
                pit = sbt(es0, "pit", [128, 1024], I32); B_pit = Buf("pit")
                ang = sbt(es0, "ang", [128, 1024]); B_ang = Buf("ang")
                kf = sbt(es0, "kf", [128, 1024]); B_kf = Buf("kf")
                kit = sbt(es0, "kit", [128, 1024], I32); B_kit = Buf("kit")
                snt = sbt(es0, "snt", [128, 1024]); B_snt = Buf("snt")
                cst = sbt(es0, "cst", [128, 1024]); B_cst = Buf("cst")

                R = slice(64, 96)
                for ch in range(TOK // 1024):
                    c0 = ch * 1024
                    S.dma("sp", pit[R, :], pos_d[0:1, c0:c0 + 1024].partition_broadcast(32), writes=[B_pit])
                    S.op("dve", lambda v: v.tensor_copy(out=ang[R, :], in_=pit[R, :]), reads=[B_pit], writes=[B_ang])
                    S.op("dve", lambda v: v.tensor_scalar(out=ang[R, :], in0=ang[R, :], scalar1=pp[R, C_INVF:C_INVF + 1],
                                                          scalar2=None, op0=ALU.mult), reads=[B_ang, B_pp], writes=[B_ang])
                    S.op("dve", lambda v: v.tensor_scalar(out=kf[R, :], in0=ang[R, :], scalar1=1.0 / TWO_PI, scalar2=None,
                                                          op0=ALU.mult), reads=[B_ang], writes=[B_kf])
                    S.op("dve", lambda v: v.tensor_copy(out=kit[R, :], in_=kf[R, :]), reads=[B_kf], writes=[B_kit])
                    S.op("dve", lambda v: v.tensor_copy(out=kf[R, :], in_=kit[R, :]), reads=[B_kit], writes=[B_kf])
                    S.op("dve", lambda v: v.scalar_tensor_tensor(out=ang[R, :], in0=kf[R, :], scalar=-CW1, in1=ang[R, :],
                                                                 op0=ALU.mult, op1=ALU.add), reads=[B_kf, B_ang], writes=[B_ang])
                    S.op("dve", lambda v: v.scalar_tensor_tensor(out=ang[R, :], in0=kf[R, :], scalar=-CW2, in1=ang[R, :],
                                                                 op0=ALU.mult, op1=ALU.add), reads=[B_kf, B_ang], writes=[B_ang])
                    S.op("dve", lambda v: v.tensor_scalar(out=kf[R, :], in0=ang[R, :], scalar1=float(np.pi), scalar2=None,
                                                          op0=ALU.is_gt), reads=[B_ang], writes=[B_kf])
                    S.op("dve", lambda v: v.scalar_tensor_tensor(out=ang[R, :], in0=kf[R, :], scalar=-TWO_PI, in1=ang[R, :],
                                                                 op0=ALU.mult, op1=ALU.add), reads=[B_kf, B_ang], writes=[B_ang])
                    S.op("dve", lambda v: v.tensor_scalar(out=ang[R, :], in0=ang[R, :], scalar1=float(np.pi), scalar2=-float(np.pi),
                                                          op0=ALU.min, op1=ALU.max), reads=[B_ang], writes=[B_ang])
                    S.op("act", lambda a: a.activation(out=snt[R, :], in_=ang[R, :], func=AF.Sin), reads=[B_ang], writes=[B_snt])
                    S.op("dve", lambda v: v.scalar_tensor_tensor(out=kf[R, :], in0=ang[R, :], scalar=-1.0, in1=ang[R, :],
                                                                 op0=ALU.mult, op1=ALU.max), reads=[B_ang], writes=[B_kf])
                    S.op("act", lambda a: a.activation(out=cst[R, :], in_=kf[R, :], func=AF.Sin, scale=-1.0,
                                                       bias=float(np.pi / 2)), reads=[B_kf], writes=[B_cst])
                    S.dma("sp", cs_d[0, :, c0:c0 + 1024], cst[R, :], reads=[B_cst])
                    S.dma("sp", cs_d[1, :, c0:c0 + 1024], snt[R, :], reads=[B_snt])

                S.op("act", lambda a: a.activation(out=cactb[:], in_=pp[:, C_CT:C_CT + 16], func=AF.Silu),
                     reads=[B_pp], writes=[B_cact])
                modps = mmb[0]
                cact3 = cactb[:].rearrange("p (k b) -> p k b", b=2)
                for jc in range(12):
                    sl = jc % 2
                    S.dma("pool", wa[sl][:], wada_d[:, :, jc * 512:(jc + 1) * 512], writes=[B_wa[sl]])

                    def f_mod(pe, jc=jc, sl=sl):
                        ins = None
                        for f in range(4):
                            j = jc * 4 + f
                            for k in range(8):
                                ins = pe.matmul(modps[:, 2 * j:2 * j + 2], lhsT=wa[sl][:, k, f * 128:(f + 1) * 128],
                                                rhs=cact3[:, k, :], start=(k == 0), stop=(k == 7))
                        return ins
                    S.op("pe", f_mod, reads=[B_wa[sl], B_cact], writes=[B_mm[0]])
                modps3 = modps[:, 0:96].rearrange("p (j b) -> p j b", b=2)

                def f_modT(v):
                    v.tensor_tensor(out=modT[:, :, 0], in0=modps3[:, :, 0], in1=pp[:, C_BADA:C_BADA + 48], op=ALU.add)
                    return v.tensor_tensor(out=modT[:, :, 1], in0=modps3[:, :, 1], in1=pp[:, C_BADA:C_BADA + 48], op=ALU.add)
                S.op("dve", f_modT, reads=[B_mm[0], B_pp], writes=[B_mod])

                def f_one(v):
                    v.tensor_scalar(out=modT[:, 8:16, :], in0=modT[:, 8:16, :], scalar1=1.0, scalar2=None, op0=ALU.add)
                    return v.tensor_scalar(out=modT[:, 32:40, :], in0=modT[:, 32:40, :], scalar1=1.0, scalar2=None, op0=ALU.add)
                S.op("dve", f_one, reads=[B_mod], writes=[B_mod])
                B_gsc = Buf("gsc")
                with nc.allow_non_contiguous_dma(reason="tiny one-time gate relayout"):
                    for b in range(NBC):
                        for wi, cbase in enumerate((16, 40)):
                            S.dma("sp", gsc_d[b * 2 + wi, :].rearrange("(k p) -> p k", p=128), modT[:, cbase:cbase + 8, b],
                                  reads=[B_mod], writes=[B_gsc], owner=B_gsc)

                S.op("dve", lambda v: v.memset(wkr[:], 0.0), writes=[B_wkr])

                def f_wkr(v):
                    v.tensor_copy(out=wkr[:, :, 0, 64:96], in_=w_in[:, :, 384:416])
                    v.tensor_scalar(out=wkr[:, :, 1, 64:80], in0=w_in[:, :, 400:416], scalar1=-1.0, scalar2=None, op0=ALU.mult)
                    return v.tensor_copy(out=wkr[:, :, 1, 80:96], in_=w_in[:, :, 384:400])
                S.op("dve", f_wkr, reads=[B_win], writes=[B_wkr])
                S.op("dve", lambda v: v.memset(wqr[:], 0.0), writes=[B_wqr])

                def f_wqr(v):
                    ins = None
                    for k in range(2):
                        wq4 = wq[:, k, :].rearrange("p (h d) -> p h d", d=96)
                        v.tensor_scalar(out=wqr[:, k, :, 64:80], in0=wq4[:, :, 80:96], scalar1=-1.0, scalar2=None, op0=ALU.mult)
                        ins = v.tensor_copy(out=wqr[:, k, :, 80:96], in_=wq4[:, :, 64:80])
                    return ins
                S.op("dve", f_wqr, reads=[B_wq], writes=[B_wqr])
                S.op("dve", lambda v: v.memset(vaug[:], 1.0), writes=[B_vinit])
                S.barrier()

            S.dma("pool", woa[:], woa_d, writes=[B_woa])
            S.dma("pool", woc[:], woc_d, writes=[B_woc])
            S.dma("sp", lnb[:, 0, :], lnp_d[0:1, :].partition_broadcast(128), writes=[B_lnb])
            S.dma("sp", lnb[:, 1, :], lnp_d[1:2, :].partition_broadcast(128), writes=[B_lnb], group=True)

            with ExitStack() as esa:
                xts = [sbt(esa, f"xt{i}", [128, 1024]) for i in range(4)]
                x_rot = Rot([(xts[i], Buf(f"x{i}")) for i in range(4)])
                x1st = [sbt(esa, f"x1st{i}", [128, 1024]) for i in range(2)]
                x1_rot = Rot([(x1st[i], Buf(f"x1st{i}")) for i in range(2)])
                hT = sbt(esa, "hT", [128, 8, T1], BF16); B_hT = Buf("hT")
                cqf = sbt(esa, "cqf", [128, 3, T1]); B_cqf = Buf("cqf")
                sq = sbt(esa, "sq", [128, 3, T1], BF16); B_sq = Buf("sq")
                rstd = sbt(esa, "rstd", [128, 2, T1]); B_rstd = Buf("rstd")
                cqn = sbt(esa, "cqn", [128, 3, T1], BF16); B_cqn = Buf("cqn")
                cos_t = sbt(esa, "cos_t", [128, T1]); sin_t = sbt(esa, "sin_t", [128, T1]); B_cs = Buf("cs")
                t1 = sbt(esa, "t1", [128, T1]); B_t1 = Buf("t1")
                t2 = sbt(esa, "t2", [128, T1]); B_t2 = Buf("t2")
                krb = sbt(esa, "krb", [128, T1], BF16); B_krb = Buf("krb")
                gcs = sbt(esa, "gcs", [128, T1]); B_gcs = Buf("gcs")
                ub = sbt(esa, "ub", [128, 4, T1 + 2]); B_u = [Buf(f"u{i}") for i in range(4)]
                cacc = sbt(esa, "cacc", [128, T1]); B_cacc = Buf("cacc")
                yc = sbt(esa, "yc", [128, T1]); B_yc = Buf("yc")
                sqc = sbt(esa, "sqc", [128, T1], BF16); B_sqc = Buf("sqc")
                rstdc = sbt(esa, "rstdc", [128, T1]); B_rstdc = Buf("rstdc")
                ycT = [sbt(esa, f"ycT{i}", [128, 4, T1], BF16) for i in range(2)]; B_ycT = [Buf(f"ycT{i}") for i in range(2)]
                qT = [sbt(esa, f"qT{i}", [128, 8, T1], BF16) for i in range(2)]
                B_qT = [[Buf(f"qT{i}_{h}") for h in range(8)] for i in range(2)]
                pTs = [sbt(esa, f"pT{i}", [128, 2 * T1], BF16) for i in range(3)]
                p_rot = Rot([(pTs[i], Buf(f"pT{i}")) for i in range(3)])
                pTd = sbt(esa, "pTd", [128, 2 * T1], BF16); B_pTd = Buf("pTd")
                osb = [sbt(esa, f"osb{i}", [128, T1]) for i in range(2)]
                osq = [sbt(esa, f"osq{i}", [128, T1], BF16) for i in range(2)]
                B_osb = [Buf(f"osb{i}") for i in range(2)]
                B_osq = [Buf(f"osq{i}") for i in range(2)]
                orstd = sbt(esa, "orstd", [128, T1]); B_orstd = Buf("orstd")
                yaT = sbt(esa, "yaT", [128, 4, T1], BF16); B_yaT = [Buf(f"yaT{h}") for h in range(8)]
                ystg = [sbt(esa, f"ystg{i}", [64, T1], BF16) for i in range(2)]; B_ystg = [Buf(f"ystg{i}") for i in range(2)]
                rs2 = sbt(esa, "rs2", [128, 2, 2]); B_rs2 = Buf("rs2")
                epst = sbt(esa, "epst", [128, 2]); B_eps = Buf("eps")
                stb_ = [sbt(esa, f"stb{i}", [128, 2, 6]) for i in range(2)]
                mvb_ = [sbt(esa, f"mvb{i}", [128, 2]) for i in range(2)]
                rsb_ = [sbt(esa, f"rsb{i}", [128, 2]) for i in range(2)]
                st_rot_b = Rot([(stb_[i], mvb_[i], rsb_[i], Buf(f"stb{i}"), Buf(f"mvb{i}"), Buf(f"rsb{i}")) for i in range(2)])

                F0, F1, F2 = mmb[0], mmb[1], mmb[2]
                B_F0, B_F2 = B_mm[0], B_mm[2]
                B_F1a = B_F1b = B_mm[1]
                mmB = Rot([(mmb[3], B_mm[3]), (mmb[4], B_mm[4]), (mmb[2], B_mm[2])])
                qkv_rot = Rot([(F0, B_F0), (F1, B_F1a)])
                accbs = [mmb[5], mmb[6]]
                B_acc = [B_mm[5], B_mm[6]]

                S.op("dve", lambda v: v.memset(pTd[:], 0.0), writes=[B_pTd])
                S.op("dve", lambda v: v.memset(epst[:, 0:1], LN_EPS), writes=[B_eps])
                S.op("dve", lambda v: v.memset(epst[:, 1:2], RMS_EPS), writes=[B_eps])

                def ln_stats_a(src, Bsrc, pool):
                    st, mv, rs, B_st, B_mv, B_rs = pool.next()

                    def f1(v):
                        v.bn_stats(out=st[:, 0, :], in_=src[:, 0:512])
                        return v.bn_stats(out=st[:, 1, :], in_=src[:, 512:1024])
                    S.op("dve", f1, reads=[Bsrc], writes=[B_st])
                    S.op("dve", lambda v: v.bn_aggr(out=mv[:], in_=st[:]), reads=[B_st], writes=[B_mv])
                    return rs, B_rs, mv, B_mv

                def ln_stats_b(h, want_nb=False):
                    rs, B_rs, mv, B_mv = h
                    S.op("act", lambda a: a.activation(out=rs[:, 0:1], in_=mv[:, 1:2], func=AF.Ln, bias=epst[:, 0:1], scale=1.0),
                         reads=[B_mv, B_eps], writes=[B_rs])
                    S.op("act", lambda a: a.activation(out=rs[:, 0:1], in_=rs[:, 0:1], func=AF.Exp, scale=-0.5),
                         reads=[B_rs], writes=[B_rs])
                    if want_nb:
                        S.op("dve", lambda v: v.scalar_tensor_tensor(out=rs[:, 1:2], in0=mv[:, 0:1], scalar=-1.0, in1=rs[:, 0:1],
                                                                     op0=ALU.mult, op1=ALU.mult), reads=[B_mv, B_rs], writes=[B_rs])

                xs_of = {}
                R = slice(64, 96)

                def front(tile):
                    b = tile // 8
                    jt = tile % 8
                    par = tile % 2
                    t0 = jt * T1
                    g0 = b * SEQ + t0
                    if jt == 0:
                        S.op("dve", lambda v: v.memset(ub[:, :, 0:2], 0.0), writes=B_u)
                    S.dma("sp", cos_t[64:96, :], cs_d[0, :, g0:g0 + T1], writes=[B_cs])
                    S.dma("sp", sin_t[64:96, :], cs_d[1, :, g0:g0 + T1], writes=[B_cs], group=True)
                    xs = []
                    for s in range(2):
                        xt, B_x = x_rot.next()
                        xs.append((xt, B_x))
                        S.dma("sp", xt[:], x_d[g0 + s * 128:g0 + (s + 1) * 128, :], writes=[B_x])
                    xs_of[tile] = xs
                    yield
                    hs = [ln_stats_a(xs[s][0], xs[s][1], st_rot) for s in range(2)]
                    for _ in range(5):
                        yield
                    for s in range(2):
                        ln_stats_b(hs[s])
                    yield
                    for s in range(2):
                        xt, B_x = xs[s]
                        rs, B_rs, mv, B_mv = hs[s]
                        xn, B_xn = xn_rot.next()
                        S.op("dve", lambda v: v.tensor_scalar(out=xn[:], in0=xt[:], scalar1=mv[:, 0:1], scalar2=rs[:, 0:1],
                                                              op0=ALU.subtract, op1=ALU.mult),
                             reads=[B_x, B_rs, B_mv], writes=[B_xn])
                        tp, B_tp = tp_rot.next()

                        def f_tp(pe):
                            ins = None
                            for k in range(8):
                                ins = pe.transpose(out=tp[:, k * 128:(k + 1) * 128], in_=xn[:, k * 128:(k + 1) * 128], identity=identb[:])
                            return ins
                        S.op("pe", f_tp, reads=[B_xn, B_idb], writes=[B_tp])
                        yield

                        def f_ev(v):
                            ins = None
                            for k in range(8):
                                ins = v.tensor_scalar(out=hT[:, k, s * 128:(s + 1) * 128], in0=tp[:, k * 128:(k + 1) * 128],
                                                      scalar1=modT[:, 8 + k, b:b + 1], scalar2=modT[:, k, b:b + 1], op0=ALU.mult, op1=ALU.add)
                            return ins
                        S.op("dve", f_ev, reads=[B_tp, B_mod], writes=[B_hT])
                        yield

                    def inproj(pe, col0, ncols, out_ap):
                        ins = None
                        for k in range(8):
                            ins = pe.matmul(out_ap, lhsT=w_in[:, k, col0:col0 + ncols], rhs=hT[:, k, :], start=(k == 0), stop=(k == 7))
                        return ins

                    bA, B_A = F0, B_F0

                    def f_A(pe):
                        inproj(pe, 0, 128, bA[:, 0:T1])
                        return inproj(pe, 128, 128, bA[:, T1:2 * T1])
                    S.op("pe", f_A, reads=[B_win, B_hT], writes=[B_A])
                    bB, B_B = F1, B_F1a
                    S.op("pe", lambda pe: inproj(pe, 256, 128, bB[:, 0:T1]), reads=[B_win, B_hT], writes=[B_B])
                    yield
                    cqf2 = cqf[:, 0:2, :].rearrange("p a t -> p (a t)")
                    sq2 = sq[:, 0:2, :].rearrange("p a t -> p (a t)")

                    def f_evA(a):
                        a.activation(out=cqf2, in_=bA[:, :], func=AF.Copy)
                        return a.activation(out=sq2, in_=bA[:, :], func=AF.Square)
                    S.op("act", f_evA, reads=[B_A], writes=[B_cqf, B_sq])

                    def f_evB(a):
                        a.activation(out=cqf[:, 2, :], in_=bB[:, 0:T1], func=AF.Copy)
                        return a.activation(out=sq[:, 2, :], in_=bB[:, 0:T1], func=AF.Square)
                    S.op("act", f_evB, reads=[B_B], writes=[B_cqf, B_sq])
                    yield
                    bC, B_C = F1, B_F1a

                    def f_C(pe):
                        pe.matmul(bC[:, 0:T1], lhsT=onesb[:], rhs=sq[:, 0, :], start=True, stop=False)
                        pe.matmul(bC[:, 0:T1], lhsT=onesb[:], rhs=sq[:, 1, :], start=False, stop=True)
                        return pe.matmul(bC[:, T1:2 * T1], lhsT=onesb[:], rhs=sq[:, 2, :], start=True, stop=True)
                    S.op("pe", f_C, reads=[B_sq, B_const], writes=[B_C])

                    bD, B_D = F0, B_F0

                    def f_D(pe):
                        ins = None
                        for r in range(2):
                            for k in range(8):
                                ins = pe.matmul(bD[0:96, r * T1:(r + 1) * T1], lhsT=wkr[:, k, r, :], rhs=hT[:, k, :],
                                                start=(k == 0), stop=(k == 7))
                        return ins
                    S.op("pe", f_D, reads=[B_wkr, B_hT], writes=[B_D])
                    yield

                    def f_rq(a):
                        a.activation(out=rstd[:, 0, :], in_=bC[:, 0:T1], func=AF.Ln, bias=epst[:, 1:2], scale=1.0 / 256)
                        return a.activation(out=rstd[:, 1, :], in_=bC[:, T1:2 * T1], func=AF.Ln, bias=epst[:, 1:2], scale=1.0 / 128)
                    S.op("act", f_rq, reads=[B_C, B_eps], writes=[B_rstd])
                    rstd2 = rstd[:].rearrange("p a t -> p (a t)")
                    S.op("act", lambda a: a.activation(out=rstd2, in_=rstd2, func=AF.Exp, scale=-0.5), reads=[B_rstd], writes=[B_rstd])
                    yield
                    S.op("dve", lambda v: v.tensor_tensor(out=t1[R, :], in0=bD[R, 0:T1], in1=cos_t[R, :], op=ALU.mult),
                         reads=[B_D, B_cs], writes=[B_t1])
                    S.op("dve", lambda v: v.tensor_tensor(out=t2[R, :], in0=bD[R, T1:2 * T1], in1=sin_t[R, :], op=ALU.mult),
                         reads=[B_D, B_cs], writes=[B_t2])
                    yield

                    def f_cqn(v):
                        v.scalar_tensor_tensor(out=cqn[:, 0, :], in0=cqf[:, 0, :], scalar=pp[:, C_QG:C_QG + 1], in1=rstd[:, 0, :],
                                               op0=ALU.mult, op1=ALU.mult)
                        v.scalar_tensor_tensor(out=cqn[:, 1, :], in0=cqf[:, 1, :], scalar=pp[:, C_QG + 1:C_QG + 2], in1=rstd[:, 0, :],
                                               op0=ALU.mult, op1=ALU.mult)
                        return v.scalar_tensor_tensor(out=cqn[:, 2, :], in0=cqf[:, 2, :], scalar=pp[:, C_KVG:C_KVG + 1], in1=rstd[:, 1, :],
                                                      op0=ALU.mult, op1=ALU.mult)
                    S.op("dve", f_cqn, reads=[B_cqf, B_rstd, B_pp], writes=[B_cqn])
                    S.op("dve", lambda v: v.tensor_tensor(out=krb[R, :], in0=t1[R, :], in1=t2[R, :], op=ALU.add),
                         reads=[B_t1, B_t2], writes=[B_krb])
                    yield

                    def f_krc(v):
                        ins = None
                        for h in range(8):
                            ins = v.tensor_copy(out=kT[R, h, t0:t0 + T1], in_=krb[R, :])
                        return ins
                    S.op("dve", f_krc, reads=[B_krb], writes=[B_kTr[jt]])
                    yield

                    def qkv_gen():
                        for h in range(8):
                            bQ, B_Q = qkv_rot.next()

                            def f_Q(pe):
                                ins = None
                                for k in range(2):
                                    ins = pe.matmul(bQ[0:96, 0:T1], lhsT=wq[:, k, h * 96:(h + 1) * 96], rhs=cqn[:, k, :], start=(k == 0), stop=(k == 1))
                                for k in range(2):
                                    ins = pe.matmul(bQ[0:96, T1:2 * T1], lhsT=wqr[:, k, h, :], rhs=cqn[:, k, :], start=(k == 0), stop=(k == 1))
                                return ins
                            S.op("pe", f_Q, reads=[B_wq, B_wqr, B_cqn], writes=[B_Q])
                            S.op("dve", lambda v: v.tensor_copy(out=qT[par][0:64, h, :], in_=bQ[0:64, 0:T1]),
                                 reads=[B_Q], writes=[B_qT[par][h]])
                            yield
                            S.op("dve", lambda v: v.tensor_tensor(out=t1[R, :], in0=bQ[R, 0:T1], in1=cos_t[R, :], op=ALU.mult),
                                 reads=[B_Q, B_cs], writes=[B_t1])
                            S.op("dve", lambda v: v.tensor_tensor(out=t2[R, :], in0=bQ[R, T1:2 * T1], in1=sin_t[R, :], op=ALU.mult),
                                 reads=[B_Q, B_cs], writes=[B_t2])
                            S.op("dve", lambda v: v.tensor_tensor(out=qT[par][R, h, :], in0=t1[R, :], in1=t2[R, :], op=ALU.add),
                                 reads=[B_t1, B_t2], writes=[B_qT[par][h]])
                            yield

                        for hp in range(4):
                            bK, B_K = qkv_rot.next()

                            def f_K(pe):
                                pe.matmul(bK[0:64, 0:T1], lhsT=wkv[:, (2 * hp) * 128:(2 * hp) * 128 + 64], rhs=cqn[:, 2, :], start=True, stop=True)
                                return pe.matmul(bK[0:64, T1:2 * T1], lhsT=wkv[:, (2 * hp + 1) * 128:(2 * hp + 1) * 128 + 64], rhs=cqn[:, 2, :],
                                                 start=True, stop=True)
                            S.op("pe", f_K, reads=[B_wkv, B_cqn], writes=[B_K])

                            def f_Kc(v):
                                v.tensor_copy(out=kT[0:64, 2 * hp, t0:t0 + T1], in_=bK[0:64, 0:T1])
                                return v.tensor_copy(out=kT[0:64, 2 * hp + 1, t0:t0 + T1], in_=bK[0:64, T1:2 * T1])
                            S.op("dve", f_Kc, reads=[B_K], writes=[B_kTn[jt][hp]])
                            yield
                        wkv_v = wkv[:].rearrange("p (h d) -> p h d", d=128)[:, :, 64:128]
                        for s in range(2):
                            bV, B_V = qkv_rot.next()
                            bV3 = bV[:, :].rearrange("p (h d) -> p h d", d=64)
                            S.op("pe", lambda pe: pe.matmul(bV3, lhsT=cqn[:, 2, s * 128:(s + 1) * 128], rhs=wkv_v, start=True, stop=True),
                                 reads=[B_wkv, B_cqn], writes=[B_V])
                            S.op("dve", lambda v: v.tensor_copy(out=vaug[:, 2 * jt + s, :, 0:64], in_=bV3),
                                 reads=[B_V, B_vinit], writes=[B_v[jt][s]])
                            yield


                    def conv_gen():
                        for c in range(4):
                            bX, B_X = F0, B_F0

                            def f_X(pe):
                                inproj(pe, 928 + c * 128, 128, bX[:, 0:T1])
                                return inproj(pe, 1440 + c * 128, 128, bX[:, T1:2 * T1])
                            S.op("pe", f_X, reads=[B_win, B_hT], writes=[B_X])
                            bY, B_Y = F1, B_F1a
                            S.op("pe", lambda pe: inproj(pe, 416 + c * 128, 128, bY[:, 0:T1]), reads=[B_win, B_hT], writes=[B_Y])
                            yield
                            S.op("dve", lambda v: v.tensor_copy(out=gcs[:], in_=bX[:, 0:T1]), reads=[B_X], writes=[B_gcs])
                            S.op("dve", lambda v: v.tensor_tensor(out=ub[:, c, 2:T1 + 2], in0=bX[:, T1:2 * T1], in1=gcs[:], op=ALU.mult),
                                 reads=[B_X, B_gcs], writes=[B_u[c]])
                            yield
                            cw = C_CW + c * 3
                            S.op("dve", lambda v: v.tensor_scalar(out=cacc[:], in0=ub[:, c, 2:T1 + 2], scalar1=pp[:, cw + 2:cw + 3],
                                                                  scalar2=pp[:, C_CB + c:C_CB + c + 1], op0=ALU.mult, op1=ALU.add),
                                 reads=[B_u[c], B_pp], writes=[B_cacc])
                            S.op("dve", lambda v: v.scalar_tensor_tensor(out=cacc[:], in0=ub[:, c, 1:T1 + 1], scalar=pp[:, cw + 1:cw + 2],
                                                                         in1=cacc[:], op0=ALU.mult, op1=ALU.add),
                                 reads=[B_u[c], B_cacc, B_pp], writes=[B_cacc])
                            yield
                            S.op("dve", lambda v: v.scalar_tensor_tensor(out=cacc[:], in0=ub[:, c, 0:T1], scalar=pp[:, cw:cw + 1],
                                                                         in1=cacc[:], op0=ALU.mult, op1=ALU.add),
                                 reads=[B_u[c], B_cacc, B_pp], writes=[B_cacc])
                            S.op("dve", lambda v: v.tensor_copy(out=ub[:, c, 0:2], in_=ub[:, c, T1:T1 + 2]), reads=[B_u[c]], writes=[B_u[c]])
                            yield
                            S.op("dve", lambda v: v.tensor_tensor(out=yc[:], in0=bY[:, 0:T1], in1=cacc[:], op=ALU.mult),
                                 reads=[B_Y, B_cacc], writes=[B_yc])
                            S.op("act", lambda a: a.activation(out=sqc[:], in_=yc[:], func=AF.Square), reads=[B_yc], writes=[B_sqc])
                            yield
                            bZ, B_Z = F1, B_F1b
                            S.op("pe", lambda pe: pe.matmul(bZ[:, T1:2 * T1], lhsT=bd[:], rhs=sqc[:], start=True, stop=True),
                                 reads=[B_sqc, B_const], writes=[B_Z])
                            for _ in range(4):
                                yield
                            S.op("act", lambda a: a.activation(out=rstdc[:], in_=bZ[:, T1:2 * T1], func=AF.Ln, bias=epst[:, 1:2], scale=1.0 / 64),
                                 reads=[B_Z, B_eps], writes=[B_rstdc])
                            yield
                            S.op("act", lambda a: a.activation(out=rstdc[:], in_=rstdc[:], func=AF.Exp, scale=-0.5), reads=[B_rstdc], writes=[B_rstdc])
                            S.op("dve", lambda v: v.scalar_tensor_tensor(out=ycT[par][:, c, :], in0=yc[:], scalar=pp[:, C_ONGC + c:C_ONGC + c + 1],
                                                                         in1=rstdc[:], op0=ALU.mult, op1=ALU.mult),
                                 reads=[B_yc, B_rstdc, B_pp], writes=[B_ycT[par]])
                            yield


                    yield from qkv_gen()
                    yield from conv_gen()

                def back(tile):
                    b = tile // 8
                    jt = tile % 8
                    par = tile % 2
                    t0 = jt * T1
                    g0 = b * SEQ + t0
                    xs = xs_of.pop(tile)
                    if jt == 0:
                        S.dma("sp", gbc[:], gsc_d[b * 2:b * 2 + 1, :].partition_broadcast(128), reads=[B_gsc], writes=[B_gbc])
                    npair = jt + 1
                    steps = [(h, pr) for h in range(8) for pr in range(npair)]

                    def emit_qk(h, pr):
                        diag = (pr == jt)
                        bS, B_S = mmB.next()
                        kt0, kt1 = 2 * pr, 2 * pr + 1
                        c1 = 128 if diag else 0
                        rd = [B_qT[par][h], B_kTn[pr][h // 2], B_kTr[pr]]

                        def f(pe):
                            ins = None
                            for _rep in range(QK_REP):
                                pe.matmul(bS[:, 0:T1], lhsT=kT[0:96, h, kt0 * 128:(kt0 + 1) * 128], rhs=qT[par][0:96, h, 0:T1], start=True, stop=True)
                                ins = pe.matmul(bS[:, T1 + c1:2 * T1], lhsT=kT[0:96, h, kt1 * 128:(kt1 + 1) * 128], rhs=qT[par][0:96, h, c1:T1],
                                                start=True, stop=True)
                            return ins
                        S.op("pe", f, reads=rd, writes=[B_S])
                        return bS, B_S, diag

                    def emit_exp_pv(h, pr, bS, B_S, diag):
                        kt0, kt1 = 2 * pr, 2 * pr + 1
                        if diag:
                            pT, B_p = pTd, B_pTd

                            def f_e(a):
                                a.activation(out=pT[0:64, 0:T1], in_=bS[0:64, 0:T1], func=AF.Exp, scale=SCALE)
                                a.activation(out=pT[0:64, T1 + 128:2 * T1], in_=bS[0:64, T1 + 128:2 * T1], func=AF.Exp, scale=SCALE)
                                a.activation(out=pT[64:128, 64:T1], in_=bS[64:128, 64:T1], func=AF.Exp, scale=SCALE)
                                return a.activation(out=pT[64:128, T1 + 192:2 * T1], in_=bS[64:128, T1 + 192:2 * T1], func=AF.Exp, scale=SCALE)
                        else:
                            pT, B_p = p_rot.next()

                            def f_e(a):
                                return a.activation(out=pT[:, :], in_=bS[:, :], func=AF.Exp, scale=SCALE)
                        S.op("act", f_e, reads=[B_S], writes=[B_p])
                        co = 0
                        accb = accbs[h % 2]
                        c1 = 128 if diag else 0
                        last = (pr == npair - 1)

                        def f_pv(pe):
                            pe.matmul(accb[0:65, co:co + T1], lhsT=vaug[:, kt0, h, :], rhs=pT[:, 0:T1], start=(pr == 0), stop=False)
                            return pe.matmul(accb[0:65, co + c1:co + T1], lhsT=vaug[:, kt1, h, :], rhs=pT[:, T1 + c1:2 * T1], start=False, stop=last)
                        S.op("pe", f_pv, reads=[B_p, B_v[pr][0], B_v[pr][1]], writes=[B_acc[h % 2]])

                    def finalize_a(h):
                        i2 = h % 2
                        co = 0
                        accb = accbs[i2]
                        S.op("dve", lambda v: v.tensor_copy(out=osb[i2][0:65, :], in_=accb[0:65, co:co + T1]), reads=[B_acc[i2]], writes=[B_osb[i2]])
                        S.op("dve", lambda v: v.tensor_tensor(out=osq[i2][0:65, :], in0=osb[i2][0:65, :], in1=osb[i2][0:65, :], op=ALU.mult),
                             reads=[B_osb[i2]], writes=[B_osq[i2]])

                    def finalize_b(h, part=0):
                        i2 = h % 2
                        co = T1
                        accb = accbs[i2]
                        if part in (0, 1):
                            S.op("pe", lambda pe: pe.matmul(accb[0:64, co:co + T1], lhsT=wn[0:65, :], rhs=osq[i2][0:65, :], start=True, stop=True),
                                 reads=[B_osq[i2], B_const], writes=[B_acc[i2]])
                        if part == 1:
                            return
                        S.op("act", lambda a: a.activation(out=orstd[0:64, :], in_=accb[0:64, co:co + T1], func=AF.Ln, scale=1.0 / 64), reads=[B_acc[i2]], writes=[B_orstd])
                        S.op("act", lambda a: a.activation(out=orstd[0:64, :], in_=orstd[0:64, :], func=AF.Exp, scale=-0.5), reads=[B_orstd], writes=[B_orstd])
                        if h % 2 == 0:
                            S.op("dve", lambda v: v.scalar_tensor_tensor(out=yaT[0:64, h // 2, :], in0=osb[i2][0:64, :], scalar=pp[0:64, C_ONGA + h:C_ONGA + h + 1],
                                                                         in1=orstd[0:64, :], op0=ALU.mult, op1=ALU.mult),
                                 reads=[B_osb[i2], B_orstd, B_pp], writes=[B_yaT[h]])
                        else:
                            k2 = (h // 2) % 2
                            S.op("dve", lambda v: v.scalar_tensor_tensor(out=ystg[k2][0:64, :], in0=osb[i2][0:64, :], scalar=pp[0:64, C_ONGA + h:C_ONGA + h + 1],
                                                                         in1=orstd[0:64, :], op0=ALU.mult, op1=ALU.mult),
                                 reads=[B_osb[i2], B_orstd, B_pp], writes=[B_ystg[k2]])
                            S.dma("sp", yaT[64:128, h // 2, :], ystg[k2][0:64, :], reads=[B_ystg[k2]], writes=[B_yaT[h]])

                    LA = 2
                    qq = [emit_qk(*steps[i0]) for i0 in range(min(LA + 1, len(steps)))]
                    nq = len(qq)
                    fa_pend = []
                    fb_pend = []

                    fc_pend = []

                    def flush(upto_step, force_head=None):
                        did = False
                        for it in list(fc_pend):
                            if it[0] <= upto_step or it[1] == force_head:
                                finalize_b(it[1], part=2); fc_pend.remove(it); did = True
                        for it in list(fb_pend):
                            if it[0] <= upto_step or it[1] == force_head:
                                fb_pend.remove(it); did = True
                                if it[1] == force_head:
                                    finalize_b(it[1])
                                else:
                                    finalize_b(it[1], part=1)
                                    fc_pend.append([upto_step + 2, it[1]])
                        for it in list(fa_pend):
                            if it[0] <= upto_step or it[1] == force_head:
                                finalize_a(it[1]); fa_pend.remove(it); did = True
                                if it[1] == force_head:
                                    finalize_b(it[1])
                                else:
                                    fb_pend.append([upto_step + 1, it[1]])
                        return did

                    for i, (h, pr) in enumerate(steps):
                        if pr == 0 and h >= 2:
                            flush(-1, force_head=h - 2)
                        cur = qq.pop(0)
                        emit_exp_pv(h, pr, *cur)
                        if nq < len(steps):
                            qq.append(emit_qk(*steps[nq]))
                            nq += 1
                        yield
                        if flush(i):
                            yield
                        if pr == npair - 1:
                            fa_pend.append([i + 1, h])
                    for it in list(fa_pend):
                        finalize_a(it[1]); fa_pend.remove(it); fb_pend.append([0, it[1]])
                    yield
                    for it in list(fb_pend):
                        finalize_b(it[1], part=1); fb_pend.remove(it); fc_pend.append([0, it[1]])
                    yield
                    for it in list(fc_pend):
                        finalize_b(it[1], part=2); fc_pend.remove(it)
                    yield

                    for s in range(2):
                        xt, B_x = xs[s]
                        x1t, B_x1 = x1_rot.next()
                        for n in range(2):
                            bO, B_O = mmB.next()

                            def f_O(pe):
                                for i in range(4):
                                    pe.matmul(bO[:, :], lhsT=yaT[:, i, s * 128:(s + 1) * 128], rhs=woa[:, i, n * 512:(n + 1) * 512],
                                              start=(i == 0), stop=False)
                                ins = None
                                for c in range(4):
                                    ins = pe.matmul(bO[:, :], lhsT=ycT[par][:, c, s * 128:(s + 1) * 128], rhs=woc[:, c, n * 512:(n + 1) * 512],
                                                    start=False, stop=(c == 3))
                                return ins
                            S.op("pe", f_O, reads=B_yaT + [B_ycT[par], B_woa, B_woc], writes=[B_O])
                            S.op("dve", lambda v: v.tensor_tensor(out=x1t[:, n * 512:(n + 1) * 512], in0=bO[:, :],
                                                                  in1=gbc[:, n * 512:(n + 1) * 512], op=ALU.mult),
                                 reads=[B_O, B_gbc], writes=[B_x1])
                            yield
                        S.op("dve", lambda v: v.scalar_tensor_tensor(out=x1t[:], in0=xt[:], scalar=ALPHA, in1=x1t[:], op0=ALU.mult, op1=ALU.add),
                             reads=[B_x, B_x1], writes=[B_x1])
                        yield
                        hh = ln_stats_a(x1t, B_x1, st_rot_b)
                        rs, B_rs, mv, B_mv = hh
                        for _ in range(5):
                            yield
                        ln_stats_b(hh)
                        yield
                        S.op("dve", lambda v: v.scalar_tensor_tensor(out=x1t[:], in0=x1t[:], scalar=mv[:, 0:1], in1=lnb[:, 0, :], op0=ALU.subtract, op1=ALU.mult),
                             reads=[B_x1, B_mv, B_lnb], writes=[B_x1])
                        yield
                        S.op("dve", lambda v: v.scalar_tensor_tensor(out=x1t[:], in0=x1t[:], scalar=rs[:, 0:1], in1=lnb[:, 1, :], op0=ALU.mult, op1=ALU.add),
                             reads=[B_x1, B_rs, B_lnb], writes=[B_x1])
                        gs = g0 + s * 128
                        S.dma("sp", x1s_d[gs:gs + 128, :], x1t[:], reads=[B_x1])
                        yield
                        hh2 = ln_stats_a(x1t, B_x1, st_rot_b)
                        rsb, B_rsb = hh2[0], hh2[1]
                        for _ in range(5):
                            yield
                        ln_stats_b(hh2, want_nb=True)
                        S.op("dve", lambda v: v.tensor_copy(out=rs2[:, s, :], in_=rsb[:]), reads=[B_rsb], writes=[B_rs2])
                        yield
                    S.dma("sp", st2_d[2 * tile:2 * tile + 2, :, :].rearrange("s p c -> p s c"), rs2[:], reads=[B_rs2])
                    yield

                def run_interleaved(gens):
                    active = [g for g in gens if g is not None]
                    while active:
                        for g in list(active):
                            try:
                                next(g)
                            except StopIteration:
                                active.remove(g)

                for b in range(NBC):
                    run_interleaved([front(b * 8)])
                    for jt in range(8):
                        tile = b * 8 + jt
                        run_interleaved([back(tile), front(tile + 1) if jt < 7 else None])
                S.barrier()

        with ExitStack() as es2:
            wdn = sbt(es2, "wdn", [128, NPAIR, 1024], BF16); B_wdn = Buf("wdn")
            wups = [sbt(es2, f"wup{i}", [128, 8, 2, 128], BF16) for i in range(6)]
            B_wups = [Buf(f"wup{i}") for i in range(6)]
            tmp = sbt(es2, "tmp", [128, 1024]); B_tmp = Buf("tmp")
            x1ts = [sbt(es2, f"x1t{i}", [128, 1024]) for i in range(8)]
            B_x1ts = [Buf(f"x1t{i}") for i in range(8)]
            h2T = [sbt(es2, f"h2T{i}", [128, 8, T2], BF16) for i in range(2)]; B_h2T = [Buf(f"h2T{i}") for i in range(2)]
            accg = [sbt(es2, f"accg{i}", [128, T2]) for i in range(2)]
            accv = [sbt(es2, f"accv{i}", [128, T2]) for i in range(2)]
            B_accg = [Buf(f"accg{i}") for i in range(2)]
            B_accv = [Buf(f"accv{i}") for i in range(2)]
            actT = [sbt(es2, f"actT{i}", [128, NPAIR, T2], BF16) for i in range(2)]
            B_actT = [[Buf(f"actT{i}_{p}") for p in range(NPAIR)] for i in range(2)]
            halo = sbt(es2, "halo", [128, 44, 2]); B_halo = [Buf(f"halo{c}") for c in range(44)]
            hcor = sbt(es2, "hcor", [128, 44, 2]); B_hcor = Buf("hcor")
            htmp = sbt(es2, "htmp", [128, 44]); B_htmp = Buf("htmp")
            st2t = [sbt(es2, f"st2t{i}", [128, 4, 2]) for i in range(2)]; B_st2 = [Buf(f"st2t{i}") for i in range(2)]
            st4 = sbt(es2, "st4", [128, 4, 2, 6]); mv4 = sbt(es2, "mv4", [128, 4, 2]); rs4 = sbt(es2, "rs4", [128, 4, 2])
            B_st4 = [Buf(f"st4_{i}") for i in range(4)]
            B_mv4 = Buf("mv4"); B_rs4 = Buf("rs4")
            mmM = Rot([(mmb[i], B_mm[i]) for i in range(4)])
            mmK = Rot([(mmb[4], B_mm[4]), (mmb[5], B_mm[5]), (mmb[6], B_mm[6])])

            S.dma("sp", lnb[:, 0, :], lnp_d[2:3, :].partition_broadcast(128), writes=[B_lnb])
            S.dma("sp", lnb[:, 1, :], lnp_d[3:4, :].partition_broadcast(128), writes=[B_lnb], group=True)
            fcw3 = pp[:, C_FCW:C_FCW + 132].rearrange("p (c k) -> p c k", k=3)

            wu_i = [0]

            def load_wup(p):
                sl = wu_i[0] % 6
                wu_i[0] += 1
                S.dma("pool", wups[sl][:].rearrange("p k g n -> p (k g n)"), wup_d[p].rearrange("p k g n -> p (k g n)"), writes=[B_wups[sl]])
                return sl
            PF = 5

            xn2 = [sbt(es2, f"xn2_{i}", [128, 1024], BF16) for i in range(4)]
            B_xn2 = [Buf(f"xn2_{i}") for i in range(4)]

            def front2(tile):
                g0 = tile * T2
                b = g0 // SEQ
                par = tile % 2
                S.dma("sp", st2t[par][:], st2_d[4 * tile:4 * tile + 4, :, :].rearrange("s p c -> p s c"), writes=[B_st2[par]])
                for s in range(4):
                    xi = par * 4 + s
                    S.dma("sp", x1ts[xi][:], x1s_d[g0 + s * 128:g0 + (s + 1) * 128, :], writes=[B_x1ts[xi]])
                yield
                for s in range(4):
                    xi = par * 4 + s
                    S.op("act", lambda a: a.activation(out=xn2[s][:], in_=x1ts[xi][:], func=AF.Identity, bias=st2t[par][:, s, 1:2], scale=st2t[par][:, s, 0:1]),
                         reads=[B_x1ts[xi], B_st2[par]], writes=[B_xn2[s]])
                    yield
                yield
                yield
                for s in range(4):
                    xn, B_xn = xn2[s], B_xn2[s]
                    tp, B_tp = tp_rot.next()

                    def f_tp(pe):
                        ins = None
                        for k in range(8):
                            ins = pe.transpose(out=tp[:, k * 128:(k + 1) * 128], in_=xn[:, k * 128:(k + 1) * 128], identity=identb[:])
                        return ins
                    S.op("pe", f_tp, reads=[B_xn, B_idb], writes=[B_tp])
                    yield

                    def f_ev(a):
                        ins = None
                        for k in range(8):
                            ins = a.activation(out=h2T[par][:, k, s * 128:(s + 1) * 128], in_=tp[:, k * 128:(k + 1) * 128], func=AF.Identity,
                                               bias=modT[:, 24 + k, b:b + 1], scale=modT[:, 32 + k, b:b + 1])
                        return ins
                    S.op("act", f_ev, reads=[B_tp, B_mod], writes=[B_h2T[par]])
                    yield
                    yield

            def main2(tile):
                g0 = tile * T2
                b = g0 // SEQ
                par = tile % 2
                if (g0 % SEQ) == 0:
                    S.op("dve", lambda v: v.memset(hcor[:], 0.0), writes=[B_hcor])
                slots = [load_wup(p) for p in range(PF)]
                for p in range(NPAIR):
                    if p + PF < NPAIR:
                        slots.append(load_wup(p + PF))
                    if tile == 0 and p < 11:
                        S.dma("pool", wdn[:, 2 * p:2 * p + 2, :], wdn_d[:, 2 * p:2 * p + 2, :], writes=[B_wdn], group=True)
                    sl = slots[p]
                    i2 = p % 2
                    bG, B_G = mmM.next()
                    bU, B_U = mmM.next()

                    def f_up(pe, g, bank):
                        ins = None
                        for k in range(8):
                            ins = pe.matmul(bank[:, :], lhsT=wups[sl][:, k, g, :], rhs=h2T[par][:, k, :], start=(k == 0), stop=(k == 7))
                        return ins
                    S.op("pe", lambda pe: f_up(pe, 0, bG), reads=[B_wups[sl], B_h2T[par]], writes=[B_G])
                    S.op("pe", lambda pe: f_up(pe, 1, bU), reads=[B_wups[sl], B_h2T[par]], writes=[B_U])
                    yield
                    for (bank, B_bank, acc, B_acc, c) in ((bG, B_G, accg[i2], B_accg[i2], p), (bU, B_U, accv[i2], B_accv[i2], NPAIR + p)):
                        S.op("act", lambda a: a.activation(out=acc[:], in_=bank[:, :], func=AF.Identity,
                                                           bias=pp[:, C_FCB + c:C_FCB + c + 1], scale=fcw3[:, c, 2:3]),
                             reads=[B_bank, B_pp], writes=[B_acc])
                        S.op("act", lambda a: a.activation(out=halo[:, c, :], in_=bank[:, T2 - 2:T2], func=AF.Copy),
                             reads=[B_bank], writes=[B_halo[c]])
                        yield
                        S.op("dve", lambda v: v.scalar_tensor_tensor(out=acc[:, 1:T2], in0=bank[:, 0:T2 - 1], scalar=fcw3[:, c, 1:2],
                                                                     in1=acc[:, 1:T2], op0=ALU.mult, op1=ALU.add),
                             reads=[B_bank, B_acc, B_pp], writes=[B_acc])
                        yield
                        S.op("dve", lambda v: v.scalar_tensor_tensor(out=acc[:, 2:T2], in0=bank[:, 0:T2 - 2], scalar=fcw3[:, c, 0:1],
                                                                     in1=acc[:, 2:T2], op0=ALU.mult, op1=ALU.add),
                             reads=[B_bank, B_acc, B_pp], writes=[B_acc])
                        yield
                        S.op("dve", lambda v: v.tensor_tensor(out=acc[:, 0:2], in0=acc[:, 0:2], in1=hcor[:, c, :], op=ALU.add),
                             reads=[B_acc, B_hcor], writes=[B_acc])
                        yield
                    S.op("act", lambda a: a.activation(out=accg[i2][:], in_=accg[i2][:], func=AF.Silu), reads=[B_accg[i2]], writes=[B_accg[i2]])
                    yield
                    S.op("dve", lambda v: v.tensor_tensor(out=actT[par][:, p, :], in0=accg[i2][:], in1=accv[i2][:], op=ALU.mult),
                         reads=[B_accg[i2], B_accv[i2]], writes=[B_actT[par][p]])
                    yield

                S.op("dve", lambda v: v.tensor_tensor(out=htmp[:], in0=halo[:, :, 0], in1=fcw3[:, :, 0], op=ALU.mult),
                     reads=B_halo + [B_pp], writes=[B_htmp])
                S.op("dve", lambda v: v.tensor_tensor(out=hcor[:, :, 1], in0=halo[:, :, 1], in1=fcw3[:, :, 0], op=ALU.mult),
                     reads=B_halo + [B_pp], writes=[B_hcor])
                S.op("dve", lambda v: v.tensor_tensor(out=hcor[:, :, 0], in0=halo[:, :, 1], in1=fcw3[:, :, 1], op=ALU.mult),
                     reads=B_halo + [B_pp, B_hcor], writes=[B_hcor])
                S.op("dve", lambda v: v.tensor_tensor(out=hcor[:, :, 0], in0=hcor[:, :, 0], in1=htmp[:], op=ALU.add),
                     reads=[B_htmp, B_hcor], writes=[B_hcor])
                yield

            def back2(tile):
                g0 = tile * T2
                b = g0 // SEQ
                par = tile % 2
                if (g0 % SEQ) == 0:
                    S.dma("sp", gbc[:], gsc_d[b * 2 + 1:b * 2 + 2, :].partition_broadcast(128), writes=[B_gbc])
                def emit_dn(s, n):
                    bO, B_O = mmK.next()

                    def f_dn(pe):
                        ins = None
                        for p in range(NPAIR):
                            ins = pe.matmul(bO[:, :], lhsT=actT[par][:, p, s * 128:(s + 1) * 128], rhs=wdn[:, p, n * 512:(n + 1) * 512],
                                            start=(p == 0), stop=(p == NPAIR - 1))
                        return ins
                    S.op("pe", f_dn, reads=B_actT[par] + [B_wdn], writes=[B_O])
                    return (s, n, bO, B_O)

                def emit_evac(item):
                    s, n, bO, B_O = item
                    xi = par * 4 + s
                    S.op("dve", lambda v: v.tensor_tensor(out=tmp[:, n * 512:(n + 1) * 512], in0=bO[:, :], in1=gbc[:, n * 512:(n + 1) * 512], op=ALU.mult),
                         reads=[B_O, B_gbc], writes=[B_tmp])
                    if n == 1:
                        S.op("dve", lambda v: v.scalar_tensor_tensor(out=x1ts[xi][:], in0=x1ts[xi][:], scalar=ALPHA, in1=tmp[:], op0=ALU.mult, op1=ALU.add),
                             reads=[B_x1ts[xi], B_tmp], writes=[B_x1ts[xi]])

                        def f_st(v):
                            v.bn_stats(out=st4[:, s, 0, :], in_=x1ts[xi][:, 0:512])
                            return v.bn_stats(out=st4[:, s, 1, :], in_=x1ts[xi][:, 512:1024])
                        S.op("dve", f_st, reads=[B_x1ts[xi]], writes=[B_st4[s]])
                        S.op("dve", lambda v: v.bn_aggr(out=mv4[:, s, :], in_=st4[:, s, :, :]), reads=[B_st4[s]], writes=[B_mv4])

                pend = None
                for s in range(4):
                    for n in range(2):
                        item = emit_dn(s, n)
                        yield
                        yield
                        if pend is not None:
                            emit_evac(pend)
                        pend = item
                        yield
                yield
                yield
                emit_evac(pend)
                yield
                S.op("act", lambda a: a.activation(out=rs4[:, :, 0], in_=mv4[:, :, 1], func=AF.Sqrt, bias=epsP2[:, 0:1], scale=1.0), reads=[B_mv4, B_epsP2], writes=[B_rs4])
                S.op("dve", lambda v: v.reciprocal(out=rs4[:, :, 0], in_=rs4[:, :, 0]), reads=[B_rs4], writes=[B_rs4])
                yield
                for s in range(4):
                    xi = par * 4 + s
                    S.op("dve", lambda v: v.scalar_tensor_tensor(out=x1ts[xi][:], in0=x1ts[xi][:], scalar=mv4[:, s, 0:1], in1=lnb[:, 0, :], op0=ALU.subtract, op1=ALU.mult),
                         reads=[B_x1ts[xi], B_mv4, B_lnb], writes=[B_x1ts[xi]])
                    yield
                    S.op("dve", lambda v: v.scalar_tensor_tensor(out=x1ts[xi][:], in0=x1ts[xi][:], scalar=rs4[:, s, 0:1], in1=lnb[:, 1, :], op0=ALU.mult, op1=ALU.add),
                         reads=[B_x1ts[xi], B_rs4, B_lnb], writes=[B_x1ts[xi]])
                    S.dma("sp", out_d[g0 + s * 128:g0 + (s + 1) * 128, :], x1ts[xi][:], reads=[B_x1ts[xi]])
                    yield

            epsP2 = sbt(es2, "epsP2", [128, 1]); B_epsP2 = Buf("epsP2")
            S.op("dve", lambda v: v.memset(epsP2[:], LN_EPS), writes=[B_epsP2])

            def run_weighted(gens):
                active = [[g, w] for g, w in gens if g is not None]
                while active:
                    for item in list(active):
                        g, w = item
                        for _ in range(w):
                            try:
                                next(g)
                            except StopIteration:
                                active.remove(item)
                                break

            run_weighted([(front2(0), 1)])
            for tile in range(NT2):
                def chain(tile=tile):
                    if tile >= 1:
                        yield from back2(tile - 1)
                    if tile + 1 < NT2:
                        yield from front2(tile + 1)
                run_weighted([(main2(tile), 4), (chain(), 1)])
            run_weighted([(back2(NT2 - 1), 1)])
            S.barrier()
    return nc


def LN_EPS_P2():
    return LN_EPS


def _bf(x):
    return np.ascontiguousarray(x, dtype=np.float32)


def prep_inputs(inp):
    x = np.asarray(inp["x"], np.float32)
    c = np.asarray(inp["c"], np.float32)
    pos = np.asarray(inp["positions"], np.int32)
    w_ada = _bf(np.asarray(inp["w_ada"])[0].reshape(8, 128, 6144).transpose(1, 0, 2))
    w_in = _bf(np.asarray(inp["w_in"])[0].reshape(8, 128, 1952).transpose(1, 0, 2))
    wq = _bf(np.asarray(inp["w_q_up"])[0].reshape(2, 128, 768).transpose(1, 0, 2))
    wkv = _bf(np.asarray(inp["w_kv_up"])[0])
    w_out = np.asarray(inp["w_out"])[0]
    woa = _bf(w_out[:512].reshape(4, 128, 1024).transpose(1, 0, 2))
    woc = _bf(w_out[512:].reshape(4, 128, 1024).transpose(1, 0, 2))
    w_up = np.asarray(inp["w_up"])[0]
    wup = _bf(w_up.reshape(8, 128, 2, NPAIR, 128).transpose(3, 1, 0, 2, 4))
    wdn = _bf(np.asarray(inp["w_down"])[0].reshape(NPAIR, 128, 1024).transpose(1, 0, 2))
    lnp = _bf(np.stack([np.asarray(inp["ln1_g"])[0], np.asarray(inp["ln1_b"])[0], np.asarray(inp["ln2_g"])[0], np.asarray(inp["ln2_b"])[0]]))
    ident = np.eye(128, dtype=np.float32)
    inv_freq = (np.float64(10000.0) ** (-(np.arange(0, 32, 2, dtype=np.float64) / 32.0))).astype(np.float32)

    ppb = np.zeros((128, NPP), np.float32)
    ppb[:, C_BADA:C_BADA + 48] = np.asarray(inp["b_ada"])[0].reshape(48, 128).T
    ppb[:, C_QG:C_QG + 2] = np.asarray(inp["q_norm_g"])[0].reshape(2, 128).T
    ppb[:, C_KVG] = np.asarray(inp["kv_norm_g"])[0]
    ong = np.asarray(inp["out_norm_g"])[0]
    ppb[0:64, C_ONGA:C_ONGA + 8] = ong[:512].reshape(8, 64).T
    ppb[:, C_ONGC:C_ONGC + 4] = ong[512:].reshape(4, 128).T
    ppb[:, C_CW:C_CW + 12] = np.asarray(inp["conv_w"])[0].reshape(3, 4, 128).transpose(2, 1, 0).reshape(128, 12)
    ppb[:, C_CB:C_CB + 4] = np.asarray(inp["conv_b"])[0].reshape(4, 128).T
    ppb[:, C_FCW:C_FCW + 132] = np.asarray(inp["ffn_conv_w"])[0].reshape(3, 44, 128).transpose(2, 1, 0).reshape(128, 132)
    ppb[:, C_FCB:C_FCB + 44] = np.asarray(inp["ffn_conv_b"])[0].reshape(44, 128).T
    ppb[:, C_INVF] = np.tile(inv_freq, 8)
    maps = []
    for core in range(8):
        pc = ppb.copy()
        cc = c[2 * core:2 * core + 2]
        pc[:, C_CT:C_CT + 16] = cc.reshape(2, 8, 128).transpose(2, 1, 0).reshape(128, 16)
        maps.append(dict(
            x=np.ascontiguousarray(x[2 * core:2 * core + 2].reshape(TOK, D)),
            pos=np.ascontiguousarray(pos[2 * core:2 * core + 2].reshape(1, TOK)),
            pp=pc, w_ada=w_ada, w_in=w_in, wq=wq, wkv=wkv, woa=woa, woc=woc, wup=wup, wdn=wdn, lnp=lnp, ident=ident))
    return maps


_NC_CACHE = {}


def kernel(**inputs):
    maps = prep_inputs(inputs)
    if "nc" not in _NC_CACHE:
        _NC_CACHE["nc"] = build()
    nc = _NC_CACHE["nc"]
    res = run_bass_kernel_spmd(nc, maps, core_ids=list(range(8)))
    out = np.stack([r["out"].reshape(NBC, SEQ, D) for r in res.results], axis=0).reshape(16, SEQ, D)
    return np.ascontiguousarray(out.astype(np.float32))
```
